# Optimizing a Trainium2 kernel written in Bass

```python
import math
import jax, jax.numpy as jnp
from jax import lax
import numpy as np

D_MODEL = 1024
BATCH = 4
SEQ = 4096
DEPTH = 2

PLE_DIM = 256
ATTN_HEAD_DIM = 64
ATTN_WIDTH = D_MODEL // 2
ATTN_HEADS = ATTN_WIDTH // ATTN_HEAD_DIM
RET_WIDTH = D_MODEL - ATTN_WIDTH
RET_HEADS = 4
RET_HEAD_DIM = RET_WIDTH // RET_HEADS
MIX_WIDTH = ATTN_WIDTH + RET_WIDTH
IN_WIDTH = 3 * ATTN_WIDTH + 4 * RET_WIDTH
SPLITS = [ATTN_WIDTH, 2 * ATTN_WIDTH, 3 * ATTN_WIDTH,
          3 * ATTN_WIDTH + RET_WIDTH, 3 * ATTN_WIDTH + 2 * RET_WIDTH,
          3 * ATTN_WIDTH + 3 * RET_WIDTH]
MOBA_BLOCK = 256
MOBA_TOPK = 3
MOBA_QCHUNK = 32
RET_CHUNK = 256
ROPE_BASE = 10000.0
D_FF = -(-8 * D_MODEL // (3 * 256)) * 256
EPS = 1e-6

kernel_name = "hymba_moba_retnet_ple_trunk"


def rms_norm(x, g):
    xf = x.astype(jnp.float32)
    y = xf * lax.rsqrt(jnp.mean(xf * xf, axis=-1, keepdims=True) + EPS)
    return (y * g.astype(jnp.float32)).astype(x.dtype)


def split_heads(t, n_heads):
    b, s, w = t.shape
    return t.reshape(b, s, n_heads, w // n_heads).transpose(0, 2, 1, 3)


def merge_heads(t):
    b, h, s, d = t.shape
    return t.transpose(0, 2, 1, 3).reshape(b, s, h * d)


def moba_attention(q, k, v):
    B, H, S, d = q.shape
    nb = S // MOBA_BLOCK
    scale = d ** -0.5
    kb = k.reshape(B, H, nb, MOBA_BLOCK, d)
    vb = v.reshape(B, H, nb, MOBA_BLOCK, d)
    k_mean = jnp.mean(kb.astype(jnp.float32), axis=3)
    gate = jnp.einsum('bhsd,bhnd->bhsn', q.astype(jnp.float32), k_mean)
    q_blk = jnp.arange(S) // MOBA_BLOCK
    past = jnp.arange(nb)[None, :] < q_blk[:, None]
    gate = jnp.where(past, gate, -jnp.inf)
    _, top_idx = lax.top_k(gate, MOBA_TOPK)
    bi = jnp.arange(B)[:, None, None, None]
    hi = jnp.arange(H)[None, :, None, None]
    n_chunks = S // MOBA_QCHUNK

    def chunk_fn(c):
        start = c * MOBA_QCHUNK
        blk = start // MOBA_BLOCK
        qc = lax.dynamic_slice_in_dim(q, start, MOBA_QCHUNK, axis=2)
        idx = lax.dynamic_slice_in_dim(top_idx, start, MOBA_QCHUNK, axis=2)
        valid = idx < blk
        kg = kb[bi, hi, idx]
        vg = vb[bi, hi, idx]
        s_sel = jnp.einsum('bhqd,bhqnkd->bhqnk', qc, kg).astype(jnp.float32) * scale
        s_sel = jnp.where(valid[..., None], s_sel, -jnp.inf)
        k_own = lax.dynamic_index_in_dim(kb, blk, axis=2, keepdims=False)
        v_own = lax.dynamic_index_in_dim(vb, blk, axis=2, keepdims=False)
        s_own = jnp.einsum('bhqd,bhkd->bhqk', qc, k_own).astype(jnp.float32) * scale
        qpos = start % MOBA_BLOCK + jnp.arange(MOBA_QCHUNK)
        kpos = jnp.arange(MOBA_BLOCK)
        s_own = jnp.where(kpos[None, :] <= qpos[:, None], s_own, -jnp.inf)
        logits = jnp.concatenate(
            [s_sel.reshape(B, H, MOBA_QCHUNK, MOBA_TOPK * MOBA_BLOCK), s_own], axis=-1)
        probs = jax.nn.softmax(logits, axis=-1)
        p_sel = probs[..., :MOBA_TOPK * MOBA_BLOCK].reshape(
            B, H, MOBA_QCHUNK, MOBA_TOPK, MOBA_BLOCK).astype(v.dtype)
        p_own = probs[..., MOBA_TOPK * MOBA_BLOCK:].astype(v.dtype)
        return (jnp.einsum('bhqnk,bhqnkd->bhqd', p_sel, vg)
                + jnp.einsum('bhqk,bhkd->bhqd', p_own, v_own))

    outs = lax.map(chunk_fn, jnp.arange(n_chunks))
    return outs.transpose(1, 2, 0, 3, 4).reshape(B, H, S, d)


def rotary(x, pos):
    d = x.shape[-1]
    inv = 1.0 / (ROPE_BASE ** jnp.linspace(0.0, 1.0, d // 2, dtype=jnp.float32))
    ang = pos[:, None].astype(jnp.float32) * inv[None, :]
    sin, cos = jnp.sin(ang), jnp.cos(ang)
    x1, x2 = x[..., 0::2], x[..., 1::2]
    out = jnp.stack([x1 * cos - x2 * sin, x1 * sin + x2 * cos], axis=-1)
    return out.reshape(x.shape)


def retention(q, k, v):
    B, H, S, dk = q.shape
    dv = v.shape[-1]
    C = RET_CHUNK
    nc = S // C
    log_g = jnp.log1p(-jnp.exp2(-5.0 - jnp.arange(H, dtype=jnp.float32)))
    qc = q.reshape(B, H, nc, C, dk)
    kc = k.reshape(B, H, nc, C, dk)
    vc = v.reshape(B, H, nc, C, dv)
    i = jnp.arange(C, dtype=jnp.float32)
    rel = i[:, None] - i[None, :]
    dmask = jnp.where(rel[None] >= 0,
                      jnp.exp(jnp.maximum(rel, 0.0)[None] * log_g[:, None, None]), 0.0)
    scores = jnp.einsum('bhnid,bhnjd->bhnij', qc, kc) * dmask[None, :, None]
    y_inner = jnp.einsum('bhnij,bhnjv->bhniv', scores, vc)
    k_dec = jnp.exp((C - 1 - i)[None, :] * log_g[:, None])
    kv = jnp.einsum('bhnjd,bhnjv->bhndv', kc * k_dec[None, :, None, :, None], vc)
    g_chunk = jnp.exp(C * log_g)[None, :, None, None]

    def step(state, kv_n):
        return state * g_chunk + kv_n, state

    _, states = lax.scan(step, jnp.zeros((B, H, dk, dv), jnp.float32),
                         kv.transpose(2, 0, 1, 3, 4))
    states = states.transpose(1, 2, 0, 3, 4)
    q_dec = jnp.exp((i + 1.0)[None, :] * log_g[:, None])
    y_cross = jnp.einsum('bhnid,bhndv->bhniv', qc * q_dec[None, :, None, :, None], states)
    return (y_inner + y_cross).reshape(B, H, S, dv)


def hybrid_layer(h, p_i, attn_norm_g, w_in, ret_norm_g, w_out, ffn_norm_g,
                 w_ffn_in, w_ffn_out, ple_norm_g, w_ple_gate, w_ple_proj):
    B, S, _ = h.shape
    s_pad = max(-(-S // MOBA_BLOCK) * MOBA_BLOCK, (MOBA_TOPK + 1) * MOBA_BLOCK)
    u = rms_norm(h, attn_norm_g) @ w_in
    u = jnp.pad(u, ((0, 0), (0, s_pad - S), (0, 0)))
    aq, ak, av, rq, rk, rv, rg = jnp.split(u, SPLITS, axis=-1)
    a = moba_attention(split_heads(aq, ATTN_HEADS), split_heads(ak, ATTN_HEADS),
                       split_heads(av, ATTN_HEADS))
    a = merge_heads(a)
    pos = jnp.arange(s_pad)
    rqh = rotary(split_heads(rq, RET_HEADS).astype(jnp.float32), pos)
    rkh = rotary(split_heads(rk, RET_HEADS).astype(jnp.float32), pos) * (RET_HEAD_DIM ** -0.5)
    r = retention(rqh, rkh, split_heads(rv, RET_HEADS).astype(jnp.float32))
    r = r * lax.rsqrt(jnp.mean(r * r, axis=-1, keepdims=True) + EPS)
    r = r * ret_norm_g.astype(jnp.float32).reshape(1, RET_HEADS, 1, RET_HEAD_DIM)
    r = (jax.nn.silu(rg.astype(jnp.float32)) * merge_heads(r)).astype(h.dtype)
    mix = jnp.concatenate([a, r], axis=-1)[:, :S]
    h = h + mix @ w_out
    z = rms_norm(h, ffn_norm_g) @ w_ffn_in
    zg, zu = jnp.split(z, [D_FF], axis=-1)
    h = h + (jax.nn.silu(zg) * zu) @ w_ffn_out
    gate = jax.nn.sigmoid(rms_norm(h, ple_norm_g) @ w_ple_gate)
    h = h + gate * (p_i @ w_ple_proj)
    return h


def setup_inputs(seed: int = 0) -> dict:
    key = jax.random.key(seed)
    ks = jax.random.split(key, 16)
    f32 = jnp.float32

    def w(k, shape, fan_in):
        return jax.random.normal(k, shape, f32) * (fan_in ** -0.5)

    def gain(k, shape):
        return 1.0 + 0.05 * jax.random.normal(k, shape, f32)

    return {
        "x": jax.random.normal(ks[0], (BATCH, SEQ, D_MODEL), f32),
        "p": jax.random.normal(ks[1], (DEPTH, BATCH, SEQ, PLE_DIM), f32),
        "attn_norm_g": gain(ks[2], (DEPTH, D_MODEL)),
        "w_in": w(ks[3], (DEPTH, D_MODEL, IN_WIDTH), D_MODEL),
        "ret_norm_g": gain(ks[4], (DEPTH, RET_WIDTH)),
        "w_out": w(ks[5], (DEPTH, MIX_WIDTH, D_MODEL), MIX_WIDTH),
        "ffn_norm_g": gain(ks[6], (DEPTH, D_MODEL)),
        "w_ffn_in": w(ks[7], (DEPTH, D_MODEL, 2 * D_FF), D_MODEL),
        "w_ffn_out": w(ks[8], (DEPTH, D_FF, D_MODEL), D_FF),
        "ple_norm_g": gain(ks[9], (DEPTH, D_MODEL)),
        "w_ple_gate": w(ks[10], (DEPTH, D_MODEL, D_MODEL), D_MODEL),
        "w_ple_proj": w(ks[11], (DEPTH, PLE_DIM, D_MODEL), PLE_DIM),
        "final_norm_g": gain(ks[12], (D_MODEL,)),
    }


def reference(x, p, attn_norm_g, w_in, ret_norm_g, w_out, ffn_norm_g, w_ffn_in,
              w_ffn_out, ple_norm_g, w_ple_gate, w_ple_proj, final_norm_g):
    h = x
    for i in range(DEPTH):
        h = hybrid_layer(h, p[i], attn_norm_g[i], w_in[i], ret_norm_g[i], w_out[i],
                         ffn_norm_g[i], w_ffn_in[i], w_ffn_out[i], ple_norm_g[i],
                         w_ple_gate[i], w_ple_proj[i])
    return rms_norm(h, final_norm_g)
```

```python
import os
import numpy as np
import ml_dtypes
import concourse.bass as bass
import concourse.mybir as mybir
from concourse.bass_utils import run_bass_kernel_spmd

F32 = mybir.dt.float32
BF16 = mybir.dt.bfloat16
ALU = mybir.AluOpType
AF = mybir.ActivationFunctionType
AX = mybir.AxisListType

D = 1024
NT = 2048
DEPTH = 2
DFF = 2816
NEG = -1.0e30
MBIG = -30000.0
EPS = 1e-6
RG_PAIRS = [[0, 1], [2, 3], [4, 5], [6, 7]]
ENGS = ("pe", "act", "dve", "pool", "sp")


class _Rec:
    def __getattr__(self, name):
        def f(*a, **k):
            self.call = (name, a, k)
            return self
        return f


class Prog:
    def __init__(self):
        self.ops = {e: [] for e in ENGS}
        self.cnt = {e: 0 for e in ENGS}
        self.waited = {e: {} for e in ENGS}
        self.lastw = {}
        self.readers = {}
        self.dcnt = {}
        self.dma_ring = [f"dg{i}" for i in range(24)]
        self.dma_ring_i = 0
        self.semnames = [e for e in ENGS if e != "sp"] + self.dma_ring + [f"w{i}" for i in range(4)]

    def _need(self, eng, tok):
        sem, val = tok
        if self.waited[eng].get(sem, 0) < val:
            self.waited[eng][sem] = val
            self.ops[eng].append(("wait", sem, val))

    def _deps(self, eng, reads, writes):
        for k in reads:
            t = self.lastw.get(k)
            if t is not None and not (t[0] == eng and eng == "pe"):
                self._need(eng, t)
            if isinstance(k, tuple) and k[0] == "ps":
                for s, v in self.readers.get(k, {}).items():
                    if s != eng:
                        self._need(eng, (s, v))
        for k in writes:
            t = self.lastw.get(k)
            if t is not None and t[0] != eng:
                self._need(eng, t)
            for s, v in self.readers.get(k, {}).items():
                if s != eng:
                    self._need(eng, (s, v))

    def _commit(self, tok, reads, writes):
        for k in reads:
            d = self.readers.setdefault(k, {})
            if d.get(tok[0], 0) < tok[1]:
                d[tok[0]] = tok[1]
        for k in writes:
            self.lastw[k] = tok
            self.readers[k] = {}

    def op(self, eng, fn, reads=(), writes=()):
        self.group(eng, [fn], reads, writes)

    def group(self, eng, fns, reads=(), writes=()):
        self._deps(eng, reads, writes)
        for i, fn in enumerate(fns):
            rec = _Rec()
            fn(rec)
            self.ops[eng].append(("op", rec.call, i == len(fns) - 1))
        self.cnt[eng] += 1
        self._commit((eng, self.cnt[eng]), reads, writes)

    def dma(self, q, out, in_, reads=(), writes=(), sem=None):
        if sem is None:
            sem = self.dma_ring[self.dma_ring_i % len(self.dma_ring)]
            self.dma_ring_i += 1
            prev = self.dcnt.get(sem, 0)
            if prev:
                self._need(q, (sem, prev))
        self._deps(q, reads, writes)
        self.dcnt[sem] = self.dcnt.get(sem, 0) + 16
        self.ops[q].append(("dma", out, in_, sem))
        self._commit((sem, self.dcnt[sem]), reads, writes)

    def collective(self, ins, outs, reads, writes, sem):
        q = "pool"
        self._deps(q, reads, writes)
        self.dcnt[sem] = self.dcnt.get(sem, 0) + 1
        self.ops[q].append(("cc", ins, outs, sem))
        self._commit((sem, self.dcnt[sem]), reads, writes)

    def barrier(self):
        toks = [(e, self.cnt[e]) for e in ENGS if e != "sp" and self.cnt[e]]
        toks += [(s, v) for s, v in self.dcnt.items() if not s.startswith("w") and s != "pool_cc"]
        for e in ENGS:
            for t in toks:
                self._need(e, t)

    def final_wait(self, eng="sp"):
        for s, v in self.dcnt.items():
            self._need(eng, (s, v))
        for e in ENGS:
            if e != "sp" and self.cnt[e]:
                self._need(eng, (e, self.cnt[e]))

    def replay(self, eng, e, sems):
        pend = []

        def flush(keep=0):
            while len(pend) > keep:
                s_, v_ = pend.pop(0)
                e.wait_ge(sems[s_], v_)

        for o in self.ops[eng]:
            if o[0] == "wait":
                pend.append((o[1], o[2]))
                continue
            if o[0] == "op":
                flush(keep=1)
                name, a, k = o[1]
                ins = getattr(e, name)(*a, **k)
                if pend:
                    s_, v_ = pend.pop(0)
                    ins._wait_ge(sems[s_], v_)
                if o[2]:
                    ins.then_inc(sems[eng], 1)
                continue
            flush()
            if False:
                pass
            elif o[0] == "dma":
                e.dma_start(out=o[1], in_=o[2]).then_inc(sems[o[3]], 16)
            elif o[0] == "cc":
                e.collective_compute("AllGather", ALU.bypass, replica_groups=RG_PAIRS,
                                     ins=[o[1]], outs=[o[2]]).then_inc(sems[o[3]])
        flush()


def build_program(layers, first, last, dbg=()):
    nc = bass.Bass("TRN2", target_bir_lowering=False)
    P = Prog()

    def ext(name, shape, dt=F32, out=False):
        return nc.dram_tensor(name, list(shape), dt, kind="ExternalOutput" if out else "ExternalInput").ap()

    xT = ext("xT", [D, NT])
    pT_d = ext("pT", [DEPTH, 256, NT])
    w_in = ext("w_in", [DEPTH, D, 3584])
    w_out = ext("w_out", [DEPTH, D, D])
    w_fi = ext("w_ffn_in", [DEPTH, D, 2 * DFF])
    w_fo = ext("w_ffn_out", [DEPTH, DFF, D])
    w_pg = ext("w_ple_gate", [DEPTH, D, D])
    w_pp = ext("w_ple_proj", [DEPTH, 256, D])
    gains_d = ext("gains", [128, 64])
    cos_d = ext("cosT", [128, NT])
    sin_d = ext("sinT", [128, NT])
    dec_d = ext("dectab", [128, 8 * 256])
    gadd_d = ext("gadd", [128, 256])
    keepn_d = ext("keepneg", [128, 256])
    dmask_d = ext("dmask", [128, 4 * 256], BF16)
    tri_d = ext("trimask", [128, 2 * 256])
    ind_d = ext("indrows", [16, 4096], BF16)
    cmat_d = ext("cmats", [128, 3 * 128], BF16)
    flag_d = ext("flag", [128, 1])
    outT = ext("outT", [D, NT], out=True)
    dbg_d = {n: ext("dbg_" + n, [D, NT], out=True) for n in dbg}

    def scr(name, shape, dt):
        return nc.dram_tensor(name, list(shape), dt).ap()

    QT = scr("scrQT", [512, NT], BF16)
    EK = scr("expK", [512, NT], BF16)
    EV = scr("expV", [1024, 1024], BF16)
    ES = scr("expS", [512, 128], F32)
    AGK = scr("agK", [1024, NT], BF16)
    AGV = scr("agV", [2048, 1024], BF16)
    AGS = scr("agS", [1024, 128], F32)
    RQ = scr("scrRQ", [512, NT], BF16)
    RK = scr("scrRK", [512, NT], BF16)
    RG = scr("scrRG", [512, NT], BF16)
    VR = scr("scrVR", [128, 16 * 512], BF16)

    off = [16512]

    def sb(name, cols, dt, at=None):
        nbytes = cols * (4 if dt == F32 else 2)
        if at is None:
            o = off[0]
            off[0] += (nbytes + 31) // 32 * 32
        else:
            o = at
        return nc.alloc_sbuf_tensor_at(name, [128, cols], dt, offset=o)

    hT = sb("hT", 8 * NT, F32)
    wsl = [sb(f"wsl{i}", 4096, BF16) for i in range(4)]
    cm = sb("cmats", 384, BF16)
    gains = sb("gains", 64, F32)
    flag = sb("flag", 1, F32)
    sloc = sb("sloc", 4 * 8 * 128, BF16)
    ARENA = off[0]
    ident = cm[:, 0:128]
    ones_bf = cm[:, 128:256]
    jmat = cm[:, 256:384]

    class Arena:
        def __init__(self):
            self.o = ARENA

        def get(self, name, cols, dt):
            nbytes = cols * (4 if dt == F32 else 2)
            t = sb(name, cols, dt, at=self.o)
            self.o += (nbytes + 31) // 32 * 32
            assert self.o <= 229344, (name, self.o)
            return t

    uid = [0]

    def nm(s):
        uid[0] += 1
        return f"{s}_{uid[0]}"

    PB = [nc.alloc_psum_tensor(f"pb{i}", [128, 512], F32) for i in range(8)]
    PT = PB[7]

    def act(out, in_, func, reads, writes, scale=1.0, bias=0.0):
        P.op("act", lambda e: e.activation(out=out, in_=in_, func=func, bias=bias, scale=scale), reads, writes)

    def tt(eng, out, in0, in1, op, reads, writes):
        P.op(eng, lambda e: e.tensor_tensor(out=out, in0=in0, in1=in1, op=op), reads, writes)

    def ts(eng, out, in0, s1, s2, op0, op1, reads, writes):
        if s2 is None:
            P.op(eng, lambda e: e.tensor_scalar(out=out, in0=in0, scalar1=s1, scalar2=None, op0=op0), reads, writes)
        else:
            P.op(eng, lambda e: e.tensor_scalar(out=out, in0=in0, scalar1=s1, scalar2=s2, op0=op0, op1=op1),
                 reads, writes)

    def stt(eng, out, in0, scalar, in1, op0, op1, reads, writes):
        P.op(eng, lambda e: e.scalar_tensor_tensor(out=out, in0=in0, scalar=scalar, in1=in1, op0=op0, op1=op1),
             reads, writes)

    def mmgroup(out, pairs, reads, writes):
        n = len(pairs)
        fns = []
        for i, (l, r) in enumerate(pairs):
            fns.append(lambda e, l=l, r=r, i=i: e.matmul(out, l, r, start=(i == 0), stop=(i == n - 1)))
        P.group("pe", fns, reads, writes)

    wloads = []
    wissued = [0]

    def wslot_view(s, k, c):
        return wsl[s][:, 0:k * c].rearrange("p (k c) -> p k c", k=k)

    def wensure(i):
        while wissued[0] < min(i + 4, len(wloads)):
            j = wissued[0]
            s = j % 4
            for (ofn, in_ap) in wloads[j]:
                P.dma("pool", ofn(s), in_ap, reads=(), writes=(("w", s),), sem=f"w{s}")
            wissued[0] += 1
        return i % 4

    def wreg(parts):
        wloads.append(parts)
        return len(wloads) - 1

    WIDX = {}
    for l in layers:
        for b, c0 in enumerate([0, 512, 1024, 2560, 1536, 2048, 3072]):
            WIDX[(l, "in", b)] = wreg([(lambda s: wslot_view(s, 8, 512),
                                        w_in[l, :, c0:c0 + 512].rearrange("(k p) c -> p k c", p=128))])
        for hf in range(2):
            WIDX[(l, "out", hf)] = wreg([(lambda s: wslot_view(s, 8, 512),
                                          w_out[l, :, hf * 512:(hf + 1) * 512].rearrange("(k p) c -> p k c", p=128))])
        for th in range(2):
            for j in range(11):
                WIDX[(l, "fi", th, j)] = wreg([
                    (lambda s: wslot_view(s, 8, 512)[:, :, 0:256],
                     w_fi[l, :, j * 256:(j + 1) * 256].rearrange("(k p) c -> p k c", p=128)),
                    (lambda s: wslot_view(s, 8, 512)[:, :, 256:512],
                     w_fi[l, :, DFF + j * 256:DFF + (j + 1) * 256].rearrange("(k p) c -> p k c", p=128))])
            for oc in range(8):
                WIDX[(l, "fo", th, oc)] = wreg([(lambda s: wslot_view(s, 22, 128),
                                                 w_fo[l, :, oc * 128:(oc + 1) * 128].rearrange("(k p) c -> p k c", p=128))])
        for hf in range(2):
            WIDX[(l, "pg", hf)] = wreg([(lambda s: wslot_view(s, 8, 512),
                                         w_pg[l, :, hf * 512:(hf + 1) * 512].rearrange("(k p) c -> p k c", p=128))])

    P.dma("sp", cm[:, :], cmat_d[:, :], writes=("cm",))
    P.dma("sp", gains[:, :], gains_d[:, :], writes=("gains",))
    P.dma("sp", flag[:, :], flag_d[:, :], writes=("flag",))
    hT3 = hT[:, :].rearrange("p (k t) -> p k t", k=8)
    for kc in range(8):
        P.dma("sp", hT3[:, kc, :], xT[kc * 128:(kc + 1) * 128, :], writes=(("h", kc),))

    def dump(name):
        if name in dbg_d:
            for kc in range(8):
                P.dma("sp", dbg_d[name][kc * 128:(kc + 1) * 128, :], hT3[:, kc, :], reads=(("h", kc),))

    def rmsnorm(A, nT3, gcol, out_f32_dram=None):
        sq = A.get(nm("sq"), 8 * 512, BF16)
        sq3 = sq[:, :].rearrange("p (k t) -> p k t", k=8)
        rs = [A.get(nm("rs"), 512, F32) for _ in range(2)]
        ob = [A.get(nm("ob"), 512, F32) for _ in range(2)] if out_f32_dram is not None else None
        for tg in range(4):
            tsl = slice(tg * 512, (tg + 1) * 512)
            for kc in range(8):
                act(sq3[:, kc, :], hT3[:, kc, tsl], AF.Square, reads=(("h", kc),), writes=(("sq", kc),))
            ps = PB[tg % 2]
            mmgroup(ps[:, :], [(ones_bf, sq3[:, kc, :]) for kc in range(8)],
                    reads=[("sq", kc) for kc in range(8)] + ["cm"], writes=(("ps", tg % 2),))
            r = rs[tg % 2]
            ts("dve", r[:, :], ps[:, :], 1.0 / D, EPS, ALU.mult, ALU.add, reads=(("ps", tg % 2),), writes=(("rs", tg % 2),))
            act(r[:, :], r[:, :], AF.Sqrt, reads=(("rs", tg % 2),), writes=(("rs", tg % 2),))
            P.op("dve", lambda e, r=r: e.reciprocal(out=r[:, :], in_=r[:, :]), reads=(("rs", tg % 2),),
                 writes=(("rs", tg % 2),))
            for kc in range(8):
                g = gains[:, gcol + kc:gcol + kc + 1]
                if out_f32_dram is None:
                    stt("dve", nT3[:, kc, tsl], hT3[:, kc, tsl], g, r[:, :], ALU.mult, ALU.mult,
                        reads=(("h", kc), ("rs", tg % 2), "gains"), writes=(("n", kc),))
                else:
                    o = ob[kc % 2]
                    stt("dve", o[:, :], hT3[:, kc, tsl], g, r[:, :], ALU.mult, ALU.mult,
                        reads=(("h", kc), ("rs", tg % 2), "gains"), writes=(("ob", kc % 2),))
                    P.dma("sp", out_f32_dram[kc * 128:(kc + 1) * 128, tsl], o[:, :], reads=(("ob", kc % 2),),
                          writes=())

    class _Stop(Exception):
        pass

    STOP = os.environ.get("KSTOP", "")

    def chk(name):
        if STOP == name:
            raise _Stop()

    try:
      for l in layers:
          li = layers.index(l)
          chk("load")
          P.barrier()
          A = Arena()
          nT = A.get(nm("nT"), 8 * NT, BF16)
          nT3 = nT[:, :].rearrange("p (k t) -> p k t", k=8)
          tabc = [A.get(nm("tabc"), 512, F32) for _ in range(2)]
          tabs = [A.get(nm("tabs"), 512, F32) for _ in range(2)]
          dect = A.get(nm("dect"), 8 * 256, F32)
          vbuf = A.get(nm("vbuf"), 16 * 512, BF16)
          evr = [A.get(nm("evr"), 512, BF16) for _ in range(4)]
          kdt = A.get(nm("kdt"), 16 * 128, BF16)
          xb = [A.get(nm("xb"), 512, BF16) for _ in range(2)]
          t1 = [A.get(nm("t1"), 512, F32) for _ in range(2)]
          t2 = [A.get(nm("t2"), 512, F32) for _ in range(2)]
          sst = A.get(nm("sst"), 128, F32)
          P.dma("sp", dect[:, :], dec_d[:, :], writes=("dect",))
          rmsnorm(A, nT3, 0 + l * 8)
          chk("norm")
          nkeys = [("n", kc) for kc in range(8)]

          evi = [0]
          pbi = [0]
          tabi = [0]

          def proj_fm(widx, sub, tg):
              s = wensure(widx)
              W = wslot_view(s, 8, 512)
              b = pbi[0] % 4
              pbi[0] += 1
              mmgroup(PB[b][:, :], [(W[:, kc, sub * 128:(sub + 1) * 128], nT3[:, kc, tg * 512:(tg + 1) * 512])
                                    for kc in range(8)],
                      reads=nkeys + [("w", s)], writes=(("ps", b),))
              return b

          def ev_out(b, dram_ap, func=AF.Copy):
              e = evi[0] % 4
              evi[0] += 1
              act(evr[e][:, :], PB[b][:, :], func, reads=(("ps", b),), writes=(("evr", e),))
              P.dma("sp", dram_ap, evr[e][:, :], reads=(("evr", e),), writes=())

          for blk, dst in ((0, QT), (1, EK)):
              for sub in range(4):
                  for tg in range(4):
                      b = proj_fm(WIDX[(l, "in", blk)], sub, tg)
                      ev_out(b, dst[sub * 128:(sub + 1) * 128, tg * 512:(tg + 1) * 512])
          chk("A1")
          vb4 = vbuf[:, :].rearrange("p (h t c) -> p h t c", h=8, t=16)
          vr3 = vbuf[:, :].rearrange("p (t c) -> p t c", t=16)
          for blk in (2, 3):
              s = wensure(WIDX[(l, "in", blk)])
              W = wslot_view(s, 8, 512)
              for t in range(16):
                  b = pbi[0] % 4
                  pbi[0] += 1
                  mmgroup(PB[b][:, :], [(nT3[:, kc, t * 128:(t + 1) * 128], W[:, kc, :]) for kc in range(8)],
                          reads=nkeys + [("w", s)], writes=(("ps", b),))
                  if blk == 2:
                      P.op("act", lambda e, b=b, t=t: e.activation(
                          out=vb4[:, :, t, :], in_=PB[b][:, :].rearrange("p (h c) -> p h c", h=8), func=AF.Copy),
                          reads=(("ps", b),), writes=("vbuf",))
                  else:
                      P.op("act", lambda e, b=b, t=t: e.activation(out=vr3[:, t, :], in_=PB[b][:, :], func=AF.Copy),
                           reads=(("ps", b),), writes=("vbuf",))
              if blk == 2:
                  P.dma("sp", EV.rearrange("(h p) c -> p h c", p=128),
                        vbuf[:, :].rearrange("p (h c) -> p h c", h=8), reads=("vbuf",), writes=("EV",))
              else:
                  P.dma("sp", VR[:, :], vbuf[:, :], reads=("vbuf",), writes=("VR",))

          chk("A2")
          def rotary(b, tg, decsl, dram_ap, keep_sb=None):
              i = tabi[0] % 2
              tabi[0] += 1
              tsl = slice(tg * 512, (tg + 1) * 512)
              P.dma("sp", tabc[i][:, :], cos_d[:, tsl], writes=(("tabc", i),))
              P.dma("sp", tabs[i][:, :], sin_d[:, tsl], writes=(("tabs", i),))
              act(xb[i][:, :], PB[b][:, :], AF.Copy, reads=(("ps", b),), writes=(("xb", i),))
              jb = 4 + i
              mmgroup(PB[jb][:, :], [(jmat, xb[i][:, :])], reads=(("xb", i), "cm"), writes=(("ps", jb),))
              tt("dve", t1[i][:, :], PB[b][:, :], tabc[i][:, :], ALU.mult, reads=(("ps", b), ("tabc", i)),
                 writes=(("t1", i),))
              tt("dve", t2[i][:, :], PB[jb][:, :], tabs[i][:, :], ALU.mult, reads=(("ps", jb), ("tabs", i)),
                 writes=(("t2", i),))
              tt("dve", t1[i][:, :], t1[i][:, :], t2[i][:, :], ALU.add, reads=(("t1", i), ("t2", i)),
                 writes=(("t1", i),))
              e = evi[0] % 4
              evi[0] += 1
              dec_b = dect[:, decsl].unsqueeze(1).to_broadcast([128, 2, 256])
              P.op("dve", lambda en: en.tensor_tensor(out=evr[e][:, :].rearrange("p (a c) -> p a c", a=2),
                                                       in0=t1[i][:, :].rearrange("p (a c) -> p a c", a=2),
                                                       in1=dec_b, op=ALU.mult),
                   reads=(("t1", i), "dect"), writes=(("evr", e),))
              P.dma("sp", dram_ap, evr[e][:, :], reads=(("evr", e),), writes=())
              return e

          for h in range(4):
              for tg in range(4):
                  b = proj_fm(WIDX[(l, "in", 4)], h, tg)
                  rotary(b, tg, slice(h * 256, (h + 1) * 256), RQ[h * 128:(h + 1) * 128, tg * 512:(tg + 1) * 512])
          chk("A3")
          kd3 = kdt[:, :].rearrange("p (t c) -> p t c", t=16)
          sl4 = sloc[:, :].rearrange("p (h n c) -> p h n c", h=4, n=8)
          gC = [float(np.exp(256.0 * np.log1p(-np.exp2(-5.0 - h)))) for h in range(4)]
          for h in range(4):
              for tg in range(4):
                  b = proj_fm(WIDX[(l, "in", 5)], h, tg)
                  e = rotary(b, tg, slice((4 + h) * 256, (5 + h) * 256),
                             RK[h * 128:(h + 1) * 128, tg * 512:(tg + 1) * 512])
                  if STOP == "A4a":
                      continue
                  fns = []
                  for a in range(4):
                      fns.append(lambda en, a=a, e=e: en.matmul(PT[:, a * 128:(a + 1) * 128],
                                                                evr[e][:, a * 128:(a + 1) * 128], ident,
                                                                start=True, stop=True))
                  P.group("pe", fns, reads=(("evr", e), "cm"), writes=(("ps", 7),))
                  P.op("dve", lambda en, tg=tg: en.tensor_copy(
                      out=kd3[:, tg * 4:(tg + 1) * 4, :], in_=PT[:, 0:512].rearrange("p (a c) -> p a c", a=4)),
                      reads=(("ps", 7),), writes=("kdt",))
              if STOP in ("A4a", "A4b"):
                  continue
              P.op("dve", lambda en: en.memset(sst[:, :], 0.0), reads=(), writes=("sst",))
              for n in range(8):
                  if n > 0:
                      P.op("dve", lambda en, n=n, h=h: en.tensor_copy(out=sl4[:, h, n, :], in_=sst[:, :]),
                           reads=("sst",), writes=(("sloc", h),))
                  mmgroup(PB[6][:, 0:128], [(kd3[:, 2 * n + j, :], vr3[:, 2 * n + j, h * 128:(h + 1) * 128])
                                            for j in range(2)],
                          reads=("kdt", "vbuf"), writes=(("ps", 6),))
                  ts("dve", sst[:, :], sst[:, :], gC[h], None, ALU.mult, ALU.bypass, reads=("sst",), writes=("sst",))
                  stt("dve", sst[:, :], PB[6][:, 0:128], gC[h], sst[:, :], ALU.mult, ALU.add,
                      reads=(("ps", 6), "sst"), writes=("sst",))
              P.dma("sp", ES[h * 128:(h + 1) * 128, :], sst[:, :], reads=("sst",), writes=("ES",))
          chk("A4")
          chk("A4a")
          chk("A4b")
          for h in range(4):
              for tg in range(4):
                  b = proj_fm(WIDX[(l, "in", 6)], h, tg)
                  ev_out(b, RG[h * 128:(h + 1) * 128, tg * 512:(tg + 1) * 512], AF.Silu)

          chk("A")
          P.barrier()
          for k_, (src, dst, key_) in enumerate(((ES, AGS, "AGS"), (EK, AGK, "AGK"), (EV, AGV, "AGV"))):
              P.collective(src.opt(), dst.opt(), reads=(), writes=(key_,), sem="pool_cc")

          chk("X")
          P.barrier()
          A = Arena()
          mixT = A.get(nm("mixT"), 8 * NT, BF16)
          mix3 = mixT[:, :].rearrange("p (k t) -> p k t", k=8)
          AB = A.o
          A.o = AB
          rqb = [A.get(nm("rqb"), NT, BF16) for _ in range(2)]
          rkb = [A.get(nm("rkb"), NT, BF16) for _ in range(2)]
          rvb = [A.get(nm("rvb"), NT, BF16) for _ in range(2)]
          rgb = [A.get(nm("rgb"), NT, BF16) for _ in range(2)]
          trim = A.get(nm("trim"), 512, F32)
          sinf = A.get(nm("sinf"), 128, F32)
          sinb = A.get(nm("sinb"), 8 * 128, BF16)
          ptr2 = [A.get(nm("ptr2"), 256, BF16) for _ in range(4)]
          ysq = [A.get(nm("ysq"), 256, BF16) for _ in range(2)]
          rst = [A.get(nm("rst"), 256, F32) for _ in range(2)]
          ytm = [A.get(nm("ytm"), 256, F32) for _ in range(2)]
          P.dma("sp", trim[:, :], tri_d[:, :], writes=("trim",))
          sinb3 = sinb[:, :].rearrange("p (n c) -> p n c", n=8)

          def ret_load(h):
              b_ = h % 2
              P.dma("sp", rqb[b_][:, :], RQ[h * 128:(h + 1) * 128, :], writes=(("rqb", b_),))
              P.dma("sp", rkb[b_][:, :], RK[h * 128:(h + 1) * 128, :], writes=(("rkb", b_),))
              P.dma("sp", rgb[b_][:, :], RG[h * 128:(h + 1) * 128, :], writes=(("rgb", b_),))
              P.dma("sp", rvb[b_][:, :].rearrange("p (t c) -> p t c", t=16),
                    VR.rearrange("p (t c) -> p t c", t=16)[:, :, h * 128:(h + 1) * 128], reads=("VR",),
                    writes=(("rvb", b_),))

          ret_load(0)
          ci = [0]
          for h in range(4):
              b_ = h % 2
              if h + 1 < 4:
                  ret_load(h + 1)
              rv3 = rvb[b_][:, :].rearrange("p (t c) -> p t c", t=16)
              P.dma("sp", sinf[:, :], AGS[h * 128:(h + 1) * 128, :], reads=("AGS",), writes=("sinf",))
              tt("dve", sinf[:, :], sinf[:, :], flag[:, 0:1].to_broadcast([128, 128]), ALU.mult,
                 reads=("sinf", "flag"), writes=("sinf",))
              for n in range(8):
                  ts("dve", sinb3[:, n, :], sinf[:, :], float(gC[h] ** n), None, ALU.mult, ALU.bypass,
                     reads=("sinf",), writes=("sinb",))
              gcol = 56 + l * 4 + h
              for n in range(8):
                  c = ci[0]
                  ci[0] += 1
                  csl = slice(n * 256, (n + 1) * 256)
                  for jt in range(2):
                      sb_, sh_ = (c % 2) * 2 + jt, 0
                      skey = ("ps", sb_)
                      mmgroup(PB[sb_][:, sh_ * 256:(sh_ + 1) * 256],
                              [(rkb[b_][:, n * 256 + jt * 128: n * 256 + (jt + 1) * 128], rqb[b_][:, csl])],
                              reads=(("rkb", b_), ("rqb", b_)), writes=(skey,))
                      pi = (c % 2) * 2 + jt
                      tt("dve", ptr2[pi][:, :], PB[sb_][:, sh_ * 256:(sh_ + 1) * 256], trim[:, jt * 256:(jt + 1) * 256],
                         ALU.mult, reads=(skey, "trim"), writes=(("ptr2", pi),))
                  yh = c % 2
                  yps = PB[4 + yh][:, 0:256]
                  ykey = ("ps", 4 + yh)
                  pairs = [(rv3[:, 2 * n + jt, :], ptr2[(c % 2) * 2 + jt][:, :]) for jt in range(2)]
                  if n > 0:
                      pairs.append((sl4[:, h, n, :], rqb[b_][:, csl]))
                  pairs.append((sinb3[:, n, :], rqb[b_][:, csl]))
                  mmgroup(yps, pairs, reads=(("rvb", b_), ("ptr2", (c % 2) * 2), ("ptr2", (c % 2) * 2 + 1),
                                             ("sloc", h), "sinb", ("rqb", b_)), writes=(ykey,))
                  act(ysq[yh][:, :], yps, AF.Square, reads=(ykey,), writes=(("ysq", yh),))
                  sps = PB[6 + yh][:, 0:256]
                  mmgroup(sps, [(ones_bf, ysq[yh][:, :])], reads=(("ysq", yh), "cm"), writes=(("ps", 6 + yh),))
                  ts("dve", rst[yh][:, :], sps, 1.0 / 128, EPS, ALU.mult, ALU.add, reads=(("ps", 6 + yh),),
                     writes=(("rst", yh),))
                  act(rst[yh][:, :], rst[yh][:, :], AF.Sqrt, reads=(("rst", yh),), writes=(("rst", yh),))
                  P.op("dve", lambda e, yh=yh: e.reciprocal(out=rst[yh][:, :], in_=rst[yh][:, :]), reads=(("rst", yh),),
                       writes=(("rst", yh),))
                  stt("dve", ytm[yh][:, :], yps, gains[:, gcol:gcol + 1], rst[yh][:, :], ALU.mult, ALU.mult,
                      reads=(ykey, ("rst", yh), "gains"), writes=(("ytm", yh),))
                  tt("dve", mix3[:, 4 + h, csl], ytm[yh][:, :], rgb[b_][:, csl], ALU.mult,
                     reads=(("ytm", yh), ("rgb", b_)), writes=(("mix", 4 + h),))

          chk("B2")
          P.barrier()
          A.o = AB
          kaug = [A.get(nm("kaug"), 4096, BF16) for _ in range(2)]
          vaug = [A.get(nm("vaug"), 32 * 128, BF16) for _ in range(2)]
          qaug = [A.get(nm("qaug"), NT, BF16) for _ in range(2)]
          ptr = [A.get(nm("ptr"), 512, BF16) for _ in range(4)]
          gaddt = A.get(nm("gaddt"), 256, F32)
          keept = A.get(nm("keept"), 256, F32)
          dmk = A.get(nm("dmk"), 4 * 256, BF16)
          gm = A.get(nm("gm"), 256, F32)
          g1 = A.get(nm("g1"), 256, F32)
          eq = A.get(nm("eq"), 256, F32)
          mx = A.get(nm("mx"), 16, F32)
          ksum = A.get(nm("ksum"), 16, F32)
          kmb = A.get(nm("kmb"), 16, BF16)
          mb = A.get(nm("mb"), 16 * 80, BF16)
          rec = [A.get(nm("rec"), 256, F32) for _ in range(2)]
          mb3 = mb[:, :].rearrange("p (q c) -> p q c", q=16)
          P.dma("sp", gaddt[:, :], gadd_d[:, :], writes=("gaddt",))
          P.dma("sp", keept[:, :], keepn_d[:, :], writes=("keept",))
          P.dma("sp", dmk[:, :], dmask_d[:, :], writes=("dmk",))
          P.op("dve", lambda en: en.memset(mb[:, :], 0.0), writes=("mb",))
          for b_ in range(2):
              P.dma("sp", kaug[b_][64:80, :], ind_d[:, :], writes=(("kaug", b_),))
              P.op("dve", lambda en, b_=b_: en.memset(vaug[b_][:, :], 1.0), writes=(("vaug", b_),))

          def moba_load(h):
              b_ = h % 2
              pp, par = h // 2, h % 2
              for r in range(2):
                  P.dma("sp", kaug[b_][0:64, r * NT:(r + 1) * NT],
                        AGK[r * 512 + pp * 128 + par * 64: r * 512 + pp * 128 + par * 64 + 64, :],
                        reads=("AGK",), writes=(("kaug", b_),))
                  P.dma("sp", vaug[b_][:, :].rearrange("p (t c) -> p t c", t=32)[:, r * 16:(r + 1) * 16,
                                                                                   par * 64:par * 64 + 64],
                        AGV[r * 1024 + h * 128: r * 1024 + (h + 1) * 128, :].rearrange("p (t c) -> p t c", t=16),
                        reads=("AGV",), writes=(("vaug", b_),))
              P.dma("sp", qaug[b_][0:64, :], QT[pp * 128 + par * 64: pp * 128 + par * 64 + 64, :],
                    reads=(), writes=(("qaug", b_),))

          def vaug_ap(b_, T, par):
              return vaug[b_][:, T * 128:(T + 1) * 128]

          moba_load(0)
          sring = [(4, 0), (5, 0), (6, 0), (7, 0)]
          for h in range(8):
              b_ = h % 2
              pp, par = h // 2, h % 2
              if h + 1 < 8:
                  moba_load(h + 1)
              kq = ("kaug", b_)
              qq = ("qaug", b_)
              P.op("dve", lambda en, b_=b_: en.tensor_reduce(
                  out=ksum[0:64, :], in_=kaug[b_][0:64, :].rearrange("p (n c) -> p n c", n=16), axis=AX.X, op=ALU.add),
                  reads=(kq,), writes=("ksum",))
              P.op("dve", lambda en: en.tensor_copy(out=kmb[0:64, :], in_=ksum[0:64, :]), reads=("ksum",),
                   writes=("kmb",))
              fns = [(lambda en, qt=qt, b_=b_: en.matmul(PB[0][:, qt * 16:(qt + 1) * 16],
                                                        qaug[b_][0:64, qt * 128:(qt + 1) * 128], kmb[0:64, :],
                                                        start=True, stop=True)) for qt in range(16)]
              P.group("pe", fns, reads=(qq, "kmb"), writes=(("ps", 0),))
              tt("dve", gm[:, :], PB[0][:, 0:256], gaddt[:, :], ALU.add, reads=(("ps", 0), "gaddt"), writes=("gm",))

              def g3(t):
                  return t[:, :].rearrange("p (q n) -> p q n", q=16)

              def mxb():
                  return mx[:, :].unsqueeze(2).to_broadcast([128, 16, 16])

              cur = gm
              ckey = "gm"
              for it in range(3):
                  P.op("dve", lambda en, cur=cur: en.tensor_reduce(out=mx[:, :], in_=g3(cur), axis=AX.X, op=ALU.max),
                       reads=(ckey,), writes=("mx",))
                  if it == 2:
                      break
                  P.op("dve", lambda en, cur=cur: en.tensor_tensor(out=g3(eq), in0=g3(cur), in1=mxb(), op=ALU.is_ge),
                       reads=(ckey, "mx"), writes=("eq",))
                  stt("dve", g1[:, :], eq[:, :], NEG, cur[:, :], ALU.mult, ALU.add, reads=("eq", ckey),
                      writes=("g1",))
                  cur = g1
                  ckey = "g1"
              ts("dve", mx[:, :], mx[:, :], -1.0e20, None, ALU.max, ALU.bypass, reads=("mx",), writes=("mx",))
              P.op("dve", lambda en: en.tensor_tensor(out=g3(eq), in0=g3(gm), in1=mxb(), op=ALU.is_lt),
                   reads=("gm", "mx"), writes=("eq",))
              P.op("dve", lambda en: en.tensor_tensor(out=mb3[:, :, 64:80], in0=g3(eq), in1=g3(keept), op=ALU.mult),
                   reads=("eq", "keept"), writes=("mb",))
              for half in range(2):
                  fns = [(lambda en, j=j, half=half: en.matmul(
                      PB[1 + j // 4][0:80, (j % 4) * 128:(j % 4 + 1) * 128], mb3[:, half * 8 + j, :], ident,
                      start=True, stop=True)) for j in range(8)]
                  P.group("pe", fns, reads=("mb", "cm"), writes=(("ps", 1), ("ps", 2)))
                  for j2 in range(2):
                      c0 = half * 1024 + j2 * 512
                      P.op("dve", lambda en, j2=j2, c0=c0, b_=b_: en.tensor_copy(
                          out=qaug[b_][64:80, c0:c0 + 512], in_=PB[1 + j2][64:80, :]),
                          reads=(("ps", 1 + j2),), writes=(qq,))
              for qb in range(8):
                  ob = 3 if qb % 2 == 0 else 1
                  ops_ = PB[ob][:, 0:256]
                  okey = ("ps", ob)
                  blocks = [(r, n) for r in range(2) for n in (range(8) if r == 0 else range(qb + 1))]
                  nbk = len(blocks)
                  qsl = slice(qb * 256, (qb + 1) * 256)

                  def s_mm(i):
                      r, n = blocks[i]
                      sb_ = sring[i % 4][0]
                      fns = []
                      for kt in range(2):
                          T = r * 16 + n * 2 + kt
                          fns.append(lambda en, T=T, kt=kt, sb_=sb_: en.matmul(
                              PB[sb_][:, kt * 256:(kt + 1) * 256], kaug[b_][0:80, T * 128:(T + 1) * 128],
                              qaug[b_][0:80, qsl], start=True, stop=True))
                      P.group("pe", fns, reads=(kq, qq), writes=(("ps", sb_),))

                  def pv(i):
                      r, n = blocks[i]
                      sb_ = sring[i % 4][0]
                      pi = i % 4
                      act(ptr[pi][:, :], PB[sb_][:, :], AF.Exp, reads=(("ps", sb_),), writes=(("ptr", pi),),
                          scale=0.125)
                      if n == qb:
                          tt("dve", ptr[pi][:, :], ptr[pi][:, :], dmk[:, r * 512:(r + 1) * 512], ALU.mult,
                             reads=(("ptr", pi), "dmk"), writes=(("ptr", pi),))
                      for kt in range(2):
                          T = r * 16 + n * 2 + kt
                          va = vaug_ap(b_, T, par)
                          P.op("pe", lambda en, va=va, pi=pi, i=i, kt=kt: en.matmul(
                              ops_, va, ptr[pi][:, kt * 256:(kt + 1) * 256], start=(i == 0 and kt == 0),
                              stop=(i == nbk - 1 and kt == 1)),
                              reads=(("vaug", b_), ("ptr", pi)), writes=(okey,))

                  s_mm(0)
                  if nbk > 1:
                      s_mm(1)
                  for i in range(nbk):
                      if i + 2 < nbk:
                          s_mm(i + 2)
                      pv(i)
                  nr = slice(0, 64) if par == 0 else slice(64, 128)
                  sr = slice(64, 128) if par == 0 else slice(0, 64)
                  rc = rec[qb % 2]
                  P.op("dve", lambda en, rc=rc, nr=nr, sr=sr: en.reciprocal(out=rc[nr, :], in_=ops_[sr, :]),
                       reads=(okey,), writes=(("rec", qb % 2),))
                  tt("dve", mix3[nr, pp, qsl], ops_[nr, :], rc[nr, :], ALU.mult, reads=(okey, ("rec", qb % 2)),
                     writes=(("mix", pp),))

          chk("B1")
          P.barrier()
          ri = [0]

          def resid_add(ps_ap, pkey, oc, tsl):
              tt("dve", hT3[:, oc, tsl], ps_ap, hT3[:, oc, tsl], ALU.add, reads=(pkey, ("h", oc)), writes=(("h", oc),))

          for hf in range(2):
              s = wensure(WIDX[(l, "out", hf)])
              W = wslot_view(s, 8, 512)
              for sub in range(4):
                  oc = hf * 4 + sub
                  for tg in range(4):
                      b = ri[0] % 7
                      ri[0] += 1
                      tsl = slice(tg * 512, (tg + 1) * 512)
                      mmgroup(PB[b][:, :], [(W[:, kc, sub * 128:(sub + 1) * 128], mix3[:, kc, tsl]) for kc in range(8)],
                              reads=[("mix", kc) for kc in range(8)] + [("w", s)], writes=(("ps", b),))
                      resid_add(PB[b][:, :], ("ps", b), oc, tsl)
          dump(f"hmix{l}")

          chk("C")
          P.barrier()
          A = Arena()
          nT = A.get(nm("nT"), 8 * NT, BF16)
          nT3 = nT[:, :].rearrange("p (k t) -> p k t", k=8)
          actT = A.get(nm("actT"), 22 * 1024, BF16)
          act3 = actT[:, :].rearrange("p (k t) -> p k t", k=22)
          sgt = [A.get(nm("sgt"), 512, F32) for _ in range(2)]
          rmsnorm(A, nT3, 16 + l * 8)
          si = [0]
          for th in range(2):
              for j in range(11):
                  s = wensure(WIDX[(l, "fi", th, j)])
                  W = wslot_view(s, 8, 512)
                  for sub in range(2):
                      for tgl in range(2):
                          tsl = slice(th * 1024 + tgl * 512, th * 1024 + (tgl + 1) * 512)
                          bg = ri[0] % 7
                          bu = (ri[0] + 1) % 7
                          ri[0] += 2
                          mmgroup(PB[bg][:, :], [(W[:, kc, sub * 128:(sub + 1) * 128], nT3[:, kc, tsl]) for kc in range(8)],
                                  reads=nkeys + [("w", s)], writes=(("ps", bg),))
                          mmgroup(PB[bu][:, :], [(W[:, kc, 256 + sub * 128:256 + (sub + 1) * 128], nT3[:, kc, tsl])
                                                 for kc in range(8)],
                                  reads=nkeys + [("w", s)], writes=(("ps", bu),))
                          sg = sgt[si[0] % 2]
                          sk = ("sgt", si[0] % 2)
                          si[0] += 1
                          act(sg[:, :], PB[bg][:, :], AF.Silu, reads=(("ps", bg),), writes=(sk,))
                          tt("dve", act3[:, j * 2 + sub, tgl * 512:(tgl + 1) * 512], PB[bu][:, :], sg[:, :], ALU.mult,
                             reads=(("ps", bu), sk), writes=(("actT", j * 2 + sub),))
              for oc in range(8):
                  s = wensure(WIDX[(l, "fo", th, oc)])
                  W = wslot_view(s, 22, 128)
                  for tgl in range(2):
                      tsl = slice(th * 1024 + tgl * 512, th * 1024 + (tgl + 1) * 512)
                      b = ri[0] % 7
                      ri[0] += 1
                      mmgroup(PB[b][:, :], [(W[:, kc, :], act3[:, kc, tgl * 512:(tgl + 1) * 512]) for kc in range(22)],
                              reads=[("actT", kc) for kc in range(22)] + [("w", s)], writes=(("ps", b),))
                      resid_add(PB[b][:, :], ("ps", b), oc, tsl)
          dump(f"hffn{l}")

          chk("FFN")
          P.barrier()
          A = Arena()
          nT = A.get(nm("nT"), 8 * NT, BF16)
          nT3 = nT[:, :].rearrange("p (k t) -> p k t", k=8)
          pTs = A.get(nm("pTs"), 2 * NT, BF16)
          pT3 = pTs[:, :].rearrange("p (k t) -> p k t", k=2)
          sgt = [A.get(nm("sgp"), 512, F32) for _ in range(2)]
          tpt = [A.get(nm("tpt"), 512, F32) for _ in range(2)]
          wpt = A.get(nm("wpt"), 2 * 1024, BF16)
          Wp = wpt[:, :].rearrange("p (k c) -> p k c", k=2)
          P.dma("pool", Wp, w_pp[l, :, :].rearrange("(k p) c -> p k c", p=128), writes=("wpt",))
          P.dma("pool", pT3, pT_d[l, :, :].rearrange("(k p) t -> p k t", p=128), writes=("pTs",))
          rmsnorm(A, nT3, 32 + l * 8)
          for hf in range(2):
              s = wensure(WIDX[(l, "pg", hf)])
              W = wslot_view(s, 8, 512)
              for sub in range(4):
                  oc = hf * 4 + sub
                  for tg in range(4):
                      tsl = slice(tg * 512, (tg + 1) * 512)
                      bg = ri[0] % 7
                      bu = (ri[0] + 1) % 7
                      ri[0] += 2
                      mmgroup(PB[bg][:, :], [(W[:, kc, sub * 128:(sub + 1) * 128], nT3[:, kc, tsl]) for kc in range(8)],
                              reads=nkeys + [("w", s)], writes=(("ps", bg),))
                      mmgroup(PB[bu][:, :], [(Wp[:, k2, oc * 128:(oc + 1) * 128], pT3[:, k2, tsl]) for k2 in range(2)],
                              reads=("pTs", "wpt"), writes=(("ps", bu),))
                      i2 = si[0] % 2
                      si[0] += 1
                      act(sgt[i2][:, :], PB[bg][:, :], AF.Sigmoid, reads=(("ps", bg),), writes=(("sgp", i2),))
                      tt("dve", tpt[i2][:, :], PB[bu][:, :], sgt[i2][:, :], ALU.mult, reads=(("ps", bu), ("sgp", i2)),
                         writes=(("tpt", i2),))
                      tt("dve", hT3[:, oc, tsl], tpt[i2][:, :], hT3[:, oc, tsl], ALU.add,
                         reads=(("tpt", i2), ("h", oc)), writes=(("h", oc),))
          dump(f"hple{l}")

    except _Stop:
        pass

    P.barrier()
    A = Arena()
    if last:
        rmsnorm(A, None, 48, out_f32_dram=outT)
    else:
        for kc in range(8):
            P.dma("sp", outT[kc * 128:(kc + 1) * 128, :], hT3[:, kc, :], reads=(("h", kc),))
    P.final_wait("sp")

    semnames = P.semnames + ["pool_cc"]
    import contextlib
    with contextlib.ExitStack() as st:
        sems = {n: st.enter_context(nc.semaphore(n)) for n in semnames}
        block = st.enter_context(nc.Block())

        @block.tensor
        def _(e):
            P.replay("pe", e, sems)

        @block.scalar
        def _(e):
            P.replay("act", e, sems)

        @block.vector
        def _(e):
            P.replay("dve", e, sems)

        @block.gpsimd
        def _(e):
            P.replay("pool", e, sems)

        @block.sync
        def _(e):
            P.replay("sp", e, sems)
    return nc


def _tables(half):
    f32 = np.float32
    pos = (half * NT + np.arange(NT)).astype(f32)
    inv = (1.0 / (f32(10000.0) ** np.linspace(0.0, 1.0, 64, dtype=f32))).astype(f32)
    ang = (pos[:, None] * inv[None, :]).astype(f32)
    cosT = np.ascontiguousarray(np.cos(ang).astype(f32).T[np.arange(128) // 2])
    sinT = np.ascontiguousarray(np.sin(ang).astype(f32).T[np.arange(128) // 2])
    logg = np.log1p(-np.exp2(-5.0 - np.arange(4, dtype=np.float64)))
    i = np.arange(256, dtype=np.float64)
    dec = np.zeros((8, 256), np.float64)
    for h in range(4):
        dec[h] = np.exp((i + 1.0) * logg[h])
        dec[4 + h] = np.exp(-(i + 1.0) * logg[h]) * (128.0 ** -0.5)
    dectab = np.ascontiguousarray(np.broadcast_to(dec.reshape(1, -1), (128, 2048))).astype(f32)
    gadd = np.zeros((16, 16), f32)
    keepn = np.full((16, 16), MBIG, f32)
    for qt in range(16):
        bg = half * 8 + qt // 2
        gadd[qt, bg:] = NEG
        keepn[qt, bg] = 0.0
    gadd = np.ascontiguousarray(np.broadcast_to(gadd.reshape(1, -1), (128, 256)))
    keepn = np.ascontiguousarray(np.broadcast_to(keepn.reshape(1, -1), (128, 256)))
    k = np.arange(128)[:, None]
    q = np.arange(256)[None, :]
    tri = [(kt * 128 + k <= q).astype(f32) for kt in range(2)]
    trimask = np.concatenate(tri, axis=1)
    dm = []
    for r in range(2):
        for kt in range(2):
            if r == half:
                dm.append(tri[kt])
            elif r < half:
                dm.append(np.ones((128, 256), f32))
            else:
                dm.append(np.zeros((128, 256), f32))
    dmask = np.concatenate(dm, axis=1).astype(ml_dtypes.bfloat16)
    ind = np.zeros((16, 4096), f32)
    for n in range(16):
        ind[n, n * 256:(n + 1) * 256] = 1.0
    ident = np.eye(128, dtype=f32)
    ones = np.ones((128, 128), f32)
    jm = np.zeros((128, 128), f32)
    for a in range(64):
        jm[2 * a + 1, 2 * a] = -1.0
        jm[2 * a, 2 * a + 1] = 1.0
    cm = np.concatenate([ident, ones, jm], axis=1).astype(ml_dtypes.bfloat16)
    flag = np.full((128, 1), float(half), f32)
    return dict(cosT=cosT, sinT=sinT, dectab=dectab, gadd=gadd, keepneg=keepn, dmask=dmask, trimask=trimask,
                indrows=ind.astype(ml_dtypes.bfloat16), cmats=cm, flag=flag)


def _gains(attn_norm_g, ffn_norm_g, ple_norm_g, final_norm_g, ret_norm_g):
    g = np.zeros((128, 64), np.float32)
    for l in range(DEPTH):
        g[:, l * 8:(l + 1) * 8] = attn_norm_g[l].reshape(8, 128).T
        g[:, 16 + l * 8:16 + (l + 1) * 8] = ffn_norm_g[l].reshape(8, 128).T
        g[:, 32 + l * 8:32 + (l + 1) * 8] = ple_norm_g[l].reshape(8, 128).T
        g[:, 56 + l * 4:56 + (l + 1) * 4] = ret_norm_g[l].reshape(4, 128).T
    g[:, 48:56] = final_norm_g.reshape(8, 128).T
    return g


_CACHE = {}


def _run(layers, first, last, xTs, shared, dbg=()):
    key = (tuple(layers), first, last, tuple(dbg))
    if key not in _CACHE:
        _CACHE[key] = build_program(list(layers), first, last, dbg)
    nc = _CACHE[key]
    in_maps = []
    for c in range(8):
        m = dict(shared)
        m.update(_tables(c % 2))
        m["xT"] = xTs[c]
        m["pT"] = shared["pT_all"][c]
        del m["pT_all"]
        in_maps.append(m)
    ncores = int(os.environ.get("KCORES", "8"))
    res = run_bass_kernel_spmd(nc, in_maps[:ncores], core_ids=list(range(ncores)))
    rr = list(res.results)
    while len(rr) < 8:
        rr.append(rr[0])
    return rr


def kernel(x, p, attn_norm_g, w_in, ret_norm_g, w_out, ffn_norm_g, w_ffn_in, w_ffn_out, ple_norm_g,
           w_ple_gate, w_ple_proj, final_norm_g, _dbg=(), _split=False):
    f32 = np.float32
    x = np.asarray(x, f32)
    p = np.asarray(p, f32)
    xTs, pTs = [], []
    for c in range(8):
        b, hf = c // 2, c % 2
        xTs.append(np.ascontiguousarray(x[b, hf * NT:(hf + 1) * NT, :].T))
        pTs.append(np.ascontiguousarray(p[:, b, hf * NT:(hf + 1) * NT, :].transpose(0, 2, 1)))
    shared = dict(
        w_in=np.asarray(w_in, f32), w_out=np.asarray(w_out, f32), w_ffn_in=np.asarray(w_ffn_in, f32),
        w_ffn_out=np.asarray(w_ffn_out, f32), w_ple_gate=np.asarray(w_ple_gate, f32),
        w_ple_proj=np.asarray(w_ple_proj, f32),
        gains=_gains(np.asarray(attn_norm_g, f32), np.asarray(ffn_norm_g, f32), np.asarray(ple_norm_g, f32),
                     np.asarray(final_norm_g, f32), np.asarray(ret_norm_g, f32)),
        pT_all=pTs,
    )
    if _split:
        r0 = _run([0], True, False, xTs, shared)
        xT1 = [np.ascontiguousarray(r["outT"]) for r in r0]
        res = _run([1], False, True, xT1, shared)
    else:
        res = _run([0, 1], True, True, xTs, shared, dbg=_dbg)
    out = np.zeros((4, 4096, D), f32)
    for c in range(8):
        b, hf = c // 2, c % 2
        out[b, hf * NT:(hf + 1) * NT, :] = res[c]["outT"].T
    if _dbg:
        return out, res
    return out
```

```python
import os
import numpy as np
import ml_dtypes
import concourse.bass as bass
import concourse.mybir as mybir
from concourse.bass_utils import run_bass_kernel_spmd

F32 = mybir.dt.float32
BF16 = mybir.dt.bfloat16
ALU = mybir.AluOpType
AF = mybir.ActivationFunctionType
AX = mybir.AxisListType

D = 1024
NT = 2048
DEPTH = 2
DFF = 2816
NEG = -1.0e30
MBIG = -30000.0
EPS = 1e-6
RG_PAIRS = [[0, 1], [2, 3], [4, 5], [6, 7]]
ENGS = ("pe", "act", "dve", "pool", "sp")


class _Rec:
    def __getattr__(self, name):
        def f(*a, **k):
            self.call = (name, a, k)
            return self
        return f


class Prog:
    def __init__(self):
        self.ops = {e: [] for e in ENGS}
        self.cnt = {e: 0 for e in ENGS}
        self.waited = {e: {} for e in ENGS}
        self.lastw = {}
        self.readers = {}
        self.dcnt = {}
        self.dma_ring = [f"dg{i}" for i in range(24)]
        self.dma_ring_i = 0
        self.semnames = [e for e in ENGS if e != "sp"] + self.dma_ring + [f"w{i}" for i in range(4)]

    def _need(self, eng, tok):
        sem, val = tok
        if self.waited[eng].get(sem, 0) < val:
            self.waited[eng][sem] = val
            self.ops[eng].append(("wait", sem, val))

    def _deps(self, eng, reads, writes):
        for k in reads:
            t = self.lastw.get(k)
            if t is not None and not (t[0] == eng and eng == "pe"):
                self._need(eng, t)
            if isinstance(k, tuple) and k[0] == "ps":
                for s, v in self.readers.get(k, {}).items():
                    if s != eng:
                        self._need(eng, (s, v))
        for k in writes:
            t = self.lastw.get(k)
            if t is not None and t[0] != eng:
                self._need(eng, t)
            for s, v in self.readers.get(k, {}).items():
                if s != eng:
                    self._need(eng, (s, v))

    def _commit(self, tok, reads, writes):
        for k in reads:
            d = self.readers.setdefault(k, {})
            if d.get(tok[0], 0) < tok[1]:
                d[tok[0]] = tok[1]
        for k in writes:
            self.lastw[k] = tok
            self.readers[k] = {}

    def op(self, eng, fn, reads=(), writes=()):
        self.group(eng, [fn], reads, writes)

    def group(self, eng, fns, reads=(), writes=()):
        self._deps(eng, reads, writes)
        for i, fn in enumerate(fns):
            rec = _Rec()
            fn(rec)
            self.ops[eng].append(("op", rec.call, i == len(fns) - 1))
        self.cnt[eng] += 1
        self._commit((eng, self.cnt[eng]), reads, writes)

    def dma(self, q, out, in_, reads=(), writes=(), sem=None):
        if sem is None:
            sem = self.dma_ring[self.dma_ring_i % len(self.dma_ring)]
            self.dma_ring_i += 1
            prev = self.dcnt.get(sem, 0)
            if prev:
                self._need(q, (sem, prev))
        self._deps(q, reads, writes)
        self.dcnt[sem] = self.dcnt.get(sem, 0) + 16
        self.ops[q].append(("dma", out, in_, sem))
        self._commit((sem, self.dcnt[sem]), reads, writes)

    def collective(self, ins, outs, reads, writes, sem):
        q = "pool"
        self._deps(q, reads, writes)
        self.dcnt[sem] = self.dcnt.get(sem, 0) + 1
        self.ops[q].append(("cc", ins, outs, sem))
        self._commit((sem, self.dcnt[sem]), reads, writes)

    def barrier(self):
        toks = [(e, self.cnt[e]) for e in ENGS if e != "sp" and self.cnt[e]]
        toks += [(s, v) for s, v in self.dcnt.items() if not s.startswith("w") and s != "pool_cc"]
        for e in ENGS:
            for t in toks:
                self._need(e, t)

    def final_wait(self, eng="sp"):
        for s, v in self.dcnt.items():
            self._need(eng, (s, v))
        for e in ENGS:
            if e != "sp" and self.cnt[e]:
                self._need(eng, (e, self.cnt[e]))

    def replay(self, eng, e, sems):
        pend = []

        def flush(keep=0):
            while len(pend) > keep:
                s_, v_ = pend.pop(0)
                e.wait_ge(sems[s_], v_)

        for o in self.ops[eng]:
            if o[0] == "wait":
                pend.append((o[1], o[2]))
                continue
            if o[0] == "op":
                flush(keep=1)
                name, a, k = o[1]
                ins = getattr(e, name)(*a, **k)
                if pend:
                    s_, v_ = pend.pop(0)
                    ins._wait_ge(sems[s_], v_)
                if o[2]:
                    ins.then_inc(sems[eng], 1)
                continue
            flush()
            if False:
                pass
            elif o[0] == "dma":
                e.dma_start(out=o[1], in_=o[2]).then_inc(sems[o[3]], 16)
            elif o[0] == "cc":
                e.collective_compute("AllGather", ALU.bypass, replica_groups=RG_PAIRS,
                                     ins=[o[1]], outs=[o[2]]).then_inc(sems[o[3]])
        flush()


def build_program(layers, first, last, dbg=()):
    nc = bass.Bass("TRN2", target_bir_lowering=False)
    P = Prog()

    def ext(name, shape, dt=F32, out=False):
        return nc.dram_tensor(name, list(shape), dt, kind="ExternalOutput" if out else "ExternalInput").ap()

    xT = ext("xT", [D, NT])
    pT_d = ext("pT", [DEPTH, 256, NT])
    w_in = ext("w_in", [DEPTH, D, 3584])
    w_out = ext("w_out", [DEPTH, D, D])
    w_fi = ext("w_ffn_in", [DEPTH, D, 2 * DFF])
    w_fo = ext("w_ffn_out", [DEPTH, DFF, D])
    w_pg = ext("w_ple_gate", [DEPTH, D, D])
    w_pp = ext("w_ple_proj", [DEPTH, 256, D])
    gains_d = ext("gains", [128, 64])
    cos_d = ext("cosT", [128, NT])
    sin_d = ext("sinT", [128, NT])
    dec_d = ext("dectab", [128, 8 * 256])
    gadd_d = ext("gadd", [128, 256])
    keepn_d = ext("keepneg", [128, 256])
    dmask_d = ext("dmask", [128, 4 * 256], BF16)
    tri_d = ext("trimask", [128, 2 * 256])
    ind_d = ext("indrows", [16, 4096], BF16)
    cmat_d = ext("cmats", [128, 3 * 128], BF16)
    flag_d = ext("flag", [128, 1])
    outT = ext("outT", [D, NT], out=True)
    dbg_d = {n: ext("dbg_" + n, [D, NT], out=True) for n in dbg}

    def scr(name, shape, dt):
        return nc.dram_tensor(name, list(shape), dt).ap()

    QT = scr("scrQT", [512, NT], BF16)
    EK = scr("expK", [512, NT], BF16)
    EV = scr("expV", [1024, 1024], BF16)
    ES = scr("expS", [512, 128], F32)
    AGK = scr("agK", [1024, NT], BF16)
    AGV = scr("agV", [2048, 1024], BF16)
    AGS = scr("agS", [1024, 128], F32)
    RQ = scr("scrRQ", [512, NT], BF16)
    RK = scr("scrRK", [512, NT], BF16)
    RG = scr("scrRG", [512, NT], BF16)
    VR = scr("scrVR", [128, 16 * 512], BF16)

    off = [16512]

    def sb(name, cols, dt, at=None):
        nbytes = cols * (4 if dt == F32 else 2)
        if at is None:
            o = off[0]
            off[0] += (nbytes + 31) // 32 * 32
        else:
            o = at
        return nc.alloc_sbuf_tensor_at(name, [128, cols], dt, offset=o)

    hT = sb("hT", 8 * NT, F32)
    wsl = [sb(f"wsl{i}", 4096, BF16) for i in range(4)]
    cm = sb("cmats", 384, BF16)
    gains = sb("gains", 64, F32)
    flag = sb("flag", 1, F32)
    sloc = sb("sloc", 4 * 8 * 128, BF16)
    ARENA = off[0]
    ident = cm[:, 0:128]
    ones_bf = cm[:, 128:256]
    jmat = cm[:, 256:384]

    class Arena:
        def __init__(self):
            self.o = ARENA

        def get(self, name, cols, dt):
            nbytes = cols * (4 if dt == F32 else 2)
            t = sb(name, cols, dt, at=self.o)
            self.o += (nbytes + 31) // 32 * 32
            assert self.o <= 229344, (name, self.o)
            return t

    uid = [0]

    def nm(s):
        uid[0] += 1
        return f"{s}_{uid[0]}"

    PB = [nc.alloc_psum_tensor(f"pb{i}", [128, 512], F32) for i in range(8)]
    PT = PB[7]

    def act(out, in_, func, reads, writes, scale=1.0, bias=0.0):
        P.op("act", lambda e: e.activation(out=out, in_=in_, func=func, bias=bias, scale=scale), reads, writes)

    def tt(eng, out, in0, in1, op, reads, writes):
        P.op(eng, lambda e: e.tensor_tensor(out=out, in0=in0, in1=in1, op=op), reads, writes)

    def ts(eng, out, in0, s1, s2, op0, op1, reads, writes):
        if s2 is None:
            P.op(eng, lambda e: e.tensor_scalar(out=out, in0=in0, scalar1=s1, scalar2=None, op0=op0), reads, writes)
        else:
            P.op(eng, lambda e: e.tensor_scalar(out=out, in0=in0, scalar1=s1, scalar2=s2, op0=op0, op1=op1),
                 reads, writes)

    def stt(eng, out, in0, scalar, in1, op0, op1, reads, writes):
        P.op(eng, lambda e: e.scalar_tensor_tensor(out=out, in0=in0, scalar=scalar, in1=in1, op0=op0, op1=op1),
             reads, writes)

    def mmgroup(out, pairs, reads, writes):
        n = len(pairs)
        fns = []
        for i, (l, r) in enumerate(pairs):
            fns.append(lambda e, l=l, r=r, i=i: e.matmul(out, l, r, start=(i == 0), stop=(i == n - 1)))
        P.group("pe", fns, reads, writes)

    wloads = []
    wissued = [0]

    def wslot_view(s, k, c):
        return wsl[s][:, 0:k * c].rearrange("p (k c) -> p k c", k=k)

    def wensure(i):
        while wissued[0] < min(i + 4, len(wloads)):
            j = wissued[0]
            s = j % 4
            for (ofn, in_ap) in wloads[j]:
                P.dma("pool", ofn(s), in_ap, reads=(), writes=(("w", s),), sem=f"w{s}")
            wissued[0] += 1
        return i % 4

    def wreg(parts):
        wloads.append(parts)
        return len(wloads) - 1

    WIDX = {}
    for l in layers:
        for b, c0 in enumerate([0, 512, 1024, 2560, 1536, 2048, 3072]):
            WIDX[(l, "in", b)] = wreg([(lambda s: wslot_view(s, 8, 512),
                                        w_in[l, :, c0:c0 + 512].rearrange("(k p) c -> p k c", p=128))])
        for hf in range(2):
            WIDX[(l, "out", hf)] = wreg([(lambda s: wslot_view(s, 8, 512),
                                          w_out[l, :, hf * 512:(hf + 1) * 512].rearrange("(k p) c -> p k c", p=128))])
        for th in range(2):
            for j in range(11):
                WIDX[(l, "fi", th, j)] = wreg([
                    (lambda s: wslot_view(s, 8, 512)[:, :, 0:256],
                     w_fi[l, :, j * 256:(j + 1) * 256].rearrange("(k p) c -> p k c", p=128)),
                    (lambda s: wslot_view(s, 8, 512)[:, :, 256:512],
                     w_fi[l, :, DFF + j * 256:DFF + (j + 1) * 256].rearrange("(k p) c -> p k c", p=128))])
            for oc in range(8):
                WIDX[(l, "fo", th, oc)] = wreg([(lambda s: wslot_view(s, 22, 128),
                                                 w_fo[l, :, oc * 128:(oc + 1) * 128].rearrange("(k p) c -> p k c", p=128))])
        for hf in range(2):
            WIDX[(l, "pg", hf)] = wreg([(lambda s: wslot_view(s, 8, 512),
                                         w_pg[l, :, hf * 512:(hf + 1) * 512].rearrange("(k p) c -> p k c", p=128))])

    P.dma("sp", cm[:, :], cmat_d[:, :], writes=("cm",))
    P.dma("sp", gains[:, :], gains_d[:, :], writes=("gains",))
    P.dma("sp", flag[:, :], flag_d[:, :], writes=("flag",))
    hT3 = hT[:, :].rearrange("p (k t) -> p k t", k=8)
    for kc in range(8):
        P.dma("sp", hT3[:, kc, :], xT[kc * 128:(kc + 1) * 128, :], writes=(("h", kc),))

    def dump(name):
        if name in dbg_d:
            for kc in range(8):
                P.dma("sp", dbg_d[name][kc * 128:(kc + 1) * 128, :], hT3[:, kc, :], reads=(("h", kc),))

    def rmsnorm(A, nT3, gcol, out_f32_dram=None):
        sq = A.get(nm("sq"), 8 * 512, BF16)
        sq3 = sq[:, :].rearrange("p (k t) -> p k t", k=8)
        rs = [A.get(nm("rs"), 512, F32) for _ in range(2)]
        ob = [A.get(nm("ob"), 512, F32) for _ in range(2)] if out_f32_dram is not None else None
        for tg in range(4):
            tsl = slice(tg * 512, (tg + 1) * 512)
            for kc in range(8):
                act(sq3[:, kc, :], hT3[:, kc, tsl], AF.Square, reads=(("h", kc),), writes=(("sq", kc),))
            ps = PB[tg % 2]
            mmgroup(ps[:, :], [(ones_bf, sq3[:, kc, :]) for kc in range(8)],
                    reads=[("sq", kc) for kc in range(8)] + ["cm"], writes=(("ps", tg % 2),))
            r = rs[tg % 2]
            ts("dve", r[:, :], ps[:, :], 1.0 / D, EPS, ALU.mult, ALU.add, reads=(("ps", tg % 2),), writes=(("rs", tg % 2),))
            act(r[:, :], r[:, :], AF.Sqrt, reads=(("rs", tg % 2),), writes=(("rs", tg % 2),))
            P.op("dve", lambda e, r=r: e.reciprocal(out=r[:, :], in_=r[:, :]), reads=(("rs", tg % 2),),
                 writes=(("rs", tg % 2),))
            for kc in range(8):
                g = gains[:, gcol + kc:gcol + kc + 1]
                if out_f32_dram is None:
                    stt("dve", nT3[:, kc, tsl], hT3[:, kc, tsl], g, r[:, :], ALU.mult, ALU.mult,
                        reads=(("h", kc), ("rs", tg % 2), "gains"), writes=(("n", kc),))
                else:
                    o = ob[kc % 2]
                    stt("dve", o[:, :], hT3[:, kc, tsl], g, r[:, :], ALU.mult, ALU.mult,
                        reads=(("h", kc), ("rs", tg % 2), "gains"), writes=(("ob", kc % 2),))
                    P.dma("sp", out_f32_dram[kc * 128:(kc + 1) * 128, tsl], o[:, :], reads=(("ob", kc % 2),),
                          writes=())

    class _Stop(Exception):
        pass

    STOP = os.environ.get("KSTOP", "")

    def chk(name):
        if STOP == name:
            raise _Stop()

    try:
      for l in layers:
          li = layers.index(l)
          chk("load")
          P.barrier()
          A = Arena()
          nT = A.get(nm("nT"), 8 * NT, BF16)
          nT3 = nT[:, :].rearrange("p (k t) -> p k t", k=8)
          tabc = [A.get(nm("tabc"), 512, F32) for _ in range(2)]
          tabs = [A.get(nm("tabs"), 512, F32) for _ in range(2)]
          dect = A.get(nm("dect"), 8 * 256, F32)
          vbuf = A.get(nm("vbuf"), 16 * 512, BF16)
          evr = [A.get(nm("evr"), 512, BF16) for _ in range(4)]
          kdt = A.get(nm("kdt"), 16 * 128, BF16)
          xb = [A.get(nm("xb"), 512, BF16) for _ in range(2)]
          t1 = [A.get(nm("t1"), 512, F32) for _ in range(2)]
          t2 = [A.get(nm("t2"), 512, F32) for _ in range(2)]
          sst = A.get(nm("sst"), 128, F32)
          P.dma("sp", dect[:, :], dec_d[:, :], writes=("dect",))
          rmsnorm(A, nT3, 0 + l * 8)
          chk("norm")
          nkeys = [("n", kc) for kc in range(8)]

          evi = [0]
          pbi = [0]
          tabi = [0]

          def proj_fm(widx, sub, tg):
              s = wensure(widx)
              W = wslot_view(s, 8, 512)
              b = pbi[0] % 4
              pbi[0] += 1
              mmgroup(PB[b][:, :], [(W[:, kc, sub * 128:(sub + 1) * 128], nT3[:, kc, tg * 512:(tg + 1) * 512])
                                    for kc in range(8)],
                      reads=nkeys + [("w", s)], writes=(("ps", b),))
              return b

          def ev_out(b, dram_ap, func=AF.Copy):
              e = evi[0] % 4
              evi[0] += 1
              act(evr[e][:, :], PB[b][:, :], func, reads=(("ps", b),), writes=(("evr", e),))
              P.dma("sp", dram_ap, evr[e][:, :], reads=(("evr", e),), writes=())

          for blk, dst in ((0, QT), (1, EK)):
              for sub in range(4):
                  for tg in range(4):
                      b = proj_fm(WIDX[(l, "in", blk)], sub, tg)
                      ev_out(b, dst[sub * 128:(sub + 1) * 128, tg * 512:(tg + 1) * 512])
          chk("A1")
          vb4 = vbuf[:, :].rearrange("p (h t c) -> p h t c", h=8, t=16)
          vr3 = vbuf[:, :].rearrange("p (t c) -> p t c", t=16)
          for blk in (2, 3):
              s = wensure(WIDX[(l, "in", blk)])
              W = wslot_view(s, 8, 512)
              for t in range(16):
                  b = pbi[0] % 4
                  pbi[0] += 1
                  mmgroup(PB[b][:, :], [(nT3[:, kc, t * 128:(t + 1) * 128], W[:, kc, :]) for kc in range(8)],
                          reads=nkeys + [("w", s)], writes=(("ps", b),))
                  if blk == 2:
                      P.op("act", lambda e, b=b, t=t: e.activation(
                          out=vb4[:, :, t, :], in_=PB[b][:, :].rearrange("p (h c) -> p h c", h=8), func=AF.Copy),
                          reads=(("ps", b),), writes=("vbuf",))
                  else:
                      P.op("act", lambda e, b=b, t=t: e.activation(out=vr3[:, t, :], in_=PB[b][:, :], func=AF.Copy),
                           reads=(("ps", b),), writes=("vbuf",))
              if blk == 2:
                  P.dma("sp", EV.rearrange("(h p) c -> p h c", p=128),
                        vbuf[:, :].rearrange("p (h c) -> p h c", h=8), reads=("vbuf",), writes=("EV",))
              else:
                  P.dma("sp", VR[:, :], vbuf[:, :], reads=("vbuf",), writes=("VR",))

          chk("A2")
          def rotary(b, tg, decsl, dram_ap, keep_sb=None):
              i = tabi[0] % 2
              tabi[0] += 1
              tsl = slice(tg * 512, (tg + 1) * 512)
              P.dma("sp", tabc[i][:, :], cos_d[:, tsl], writes=(("tabc", i),))
              P.dma("sp", tabs[i][:, :], sin_d[:, tsl], writes=(("tabs", i),))
              act(xb[i][:, :], PB[b][:, :], AF.Copy, reads=(("ps", b),), writes=(("xb", i),))
              jb = 4 + i
              mmgroup(PB[jb][:, :], [(jmat, xb[i][:, :])], reads=(("xb", i), "cm"), writes=(("ps", jb),))
              tt("dve", t1[i][:, :], PB[b][:, :], tabc[i][:, :], ALU.mult, reads=(("ps", b), ("tabc", i)),
                 writes=(("t1", i),))
              tt("dve", t2[i][:, :], PB[jb][:, :], tabs[i][:, :], ALU.mult, reads=(("ps", jb), ("tabs", i)),
                 writes=(("t2", i),))
              tt("dve", t1[i][:, :], t1[i][:, :], t2[i][:, :], ALU.add, reads=(("t1", i), ("t2", i)),
                 writes=(("t1", i),))
              e = evi[0] % 4
              evi[0] += 1
              dec_b = dect[:, decsl].unsqueeze(1).to_broadcast([128, 2, 256])
              P.op("dve", lambda en: en.tensor_tensor(out=evr[e][:, :].rearrange("p (a c) -> p a c", a=2),
                                                       in0=t1[i][:, :].rearrange("p (a c) -> p a c", a=2),
                                                       in1=dec_b, op=ALU.mult),
                   reads=(("t1", i), "dect"), writes=(("evr", e),))
              P.dma("sp", dram_ap, evr[e][:, :], reads=(("evr", e),), writes=())
              return e

          for h in range(4):
              for tg in range(4):
                  b = proj_fm(WIDX[(l, "in", 4)], h, tg)
                  rotary(b, tg, slice(h * 256, (h + 1) * 256), RQ[h * 128:(h + 1) * 128, tg * 512:(tg + 1) * 512])
          chk("A3")
          kd3 = kdt[:, :].rearrange("p (t c) -> p t c", t=16)
          sl4 = sloc[:, :].rearrange("p (h n c) -> p h n c", h=4, n=8)
          gC = [float(np.exp(256.0 * np.log1p(-np.exp2(-5.0 - h)))) for h in range(4)]
          for h in range(4):
              for tg in range(4):
                  b = proj_fm(WIDX[(l, "in", 5)], h, tg)
                  e = rotary(b, tg, slice((4 + h) * 256, (5 + h) * 256),
                             RK[h * 128:(h + 1) * 128, tg * 512:(tg + 1) * 512])
                  if STOP == "A4a":
                      continue
                  fns = []
                  for a in range(4):
                      fns.append(lambda en, a=a, e=e: en.matmul(PT[:, a * 128:(a + 1) * 128],
                                                                evr[e][:, a * 128:(a + 1) * 128], ident,
                                                                start=True, stop=True))
                  P.group("pe", fns, reads=(("evr", e), "cm"), writes=(("ps", 7),))
                  P.op("dve", lambda en, tg=tg: en.tensor_copy(
                      out=kd3[:, tg * 4:(tg + 1) * 4, :], in_=PT[:, 0:512].rearrange("p (a c) -> p a c", a=4)),
                      reads=(("ps", 7),), writes=("kdt",))
              if STOP in ("A4a", "A4b"):
                  continue
              P.op("dve", lambda en: en.memset(sst[:, :], 0.0), reads=(), writes=("sst",))
              for n in range(8):
                  if n > 0:
                      P.op("dve", lambda en, n=n, h=h: en.tensor_copy(out=sl4[:, h, n, :], in_=sst[:, :]),
                           reads=("sst",), writes=(("sloc", h),))
                  mmgroup(PB[6][:, 0:128], [(kd3[:, 2 * n + j, :], vr3[:, 2 * n + j, h * 128:(h + 1) * 128])
                                            for j in range(2)],
                          reads=("kdt", "vbuf"), writes=(("ps", 6),))
                  ts("dve", sst[:, :], sst[:, :], gC[h], None, ALU.mult, ALU.bypass, reads=("sst",), writes=("sst",))
                  stt("dve", sst[:, :], PB[6][:, 0:128], gC[h], sst[:, :], ALU.mult, ALU.add,
                      reads=(("ps", 6), "sst"), writes=("sst",))
              P.dma("sp", ES[h * 128:(h + 1) * 128, :], sst[:, :], reads=("sst",), writes=("ES",))
          chk("A4")
          chk("A4a")
          chk("A4b")
          for h in range(4):
              for tg in range(4):
                  b = proj_fm(WIDX[(l, "in", 6)], h, tg)
                  ev_out(b, RG[h * 128:(h + 1) * 128, tg * 512:(tg + 1) * 512], AF.Silu)

          chk("A")
          P.barrier()
          for k_, (src, dst, key_) in enumerate(((ES, AGS, "AGS"), (EK, AGK, "AGK"), (EV, AGV, "AGV"))):
              P.collective(src.opt(), dst.opt(), reads=(), writes=(key_,), sem="pool_cc")

          chk("X")
          P.barrier()
          A = Arena()
          mixT = A.get(nm("mixT"), 8 * NT, BF16)
          mix3 = mixT[:, :].rearrange("p (k t) -> p k t", k=8)
          AB = A.o
          A.o = AB
          rqb = [A.get(nm("rqb"), NT, BF16) for _ in range(2)]
          rkb = [A.get(nm("rkb"), NT, BF16) for _ in range(2)]
          rvb = [A.get(nm("rvb"), NT, BF16) for _ in range(2)]
          rgb = [A.get(nm("rgb"), NT, BF16) for _ in range(2)]
          trim = A.get(nm("trim"), 512, F32)
          sinf = A.get(nm("sinf"), 128, F32)
          sinb = A.get(nm("sinb"), 8 * 128, BF16)
          ptr2 = [A.get(nm("ptr2"), 256, BF16) for _ in range(4)]
          ysq = [A.get(nm("ysq"), 256, BF16) for _ in range(2)]
          rst = [A.get(nm("rst"), 256, F32) for _ in range(2)]
          ytm = [A.get(nm("ytm"), 256, F32) for _ in range(2)]
          P.dma("sp", trim[:, :], tri_d[:, :], writes=("trim",))
          sinb3 = sinb[:, :].rearrange("p (n c) -> p n c", n=8)

          def ret_load(h):
              b_ = h % 2
              P.dma("sp", rqb[b_][:, :], RQ[h * 128:(h + 1) * 128, :], writes=(("rqb", b_),))
              P.dma("sp", rkb[b_][:, :], RK[h * 128:(h + 1) * 128, :], writes=(("rkb", b_),))
              P.dma("sp", rgb[b_][:, :], RG[h * 128:(h + 1) * 128, :], writes=(("rgb", b_),))
              P.dma("sp", rvb[b_][:, :].rearrange("p (t c) -> p t c", t=16),
                    VR.rearrange("p (t c) -> p t c", t=16)[:, :, h * 128:(h + 1) * 128], reads=("VR",),
                    writes=(("rvb", b_),))

          ret_load(0)
          ci = [0]
          for h in range(4):
              b_ = h % 2
              if h + 1 < 4:
                  ret_load(h + 1)
              rv3 = rvb[b_][:, :].rearrange("p (t c) -> p t c", t=16)
              P.dma("sp", sinf[:, :], AGS[h * 128:(h + 1) * 128, :], reads=("AGS",), writes=("sinf",))
              tt("dve", sinf[:, :], sinf[:, :], flag[:, 0:1].to_broadcast([128, 128]), ALU.mult,
                 reads=("sinf", "flag"), writes=("sinf",))
              for n in range(8):
                  ts("dve", sinb3[:, n, :], sinf[:, :], float(gC[h] ** n), None, ALU.mult, ALU.bypass,
                     reads=("sinf",), writes=("sinb",))
              gcol = 56 + l * 4 + h
              for n in range(8):
                  c = ci[0]
                  ci[0] += 1
                  csl = slice(n * 256, (n + 1) * 256)
                  for jt in range(2):
                      sb_, sh_ = (c % 2) * 2 + jt, 0
                      skey = ("ps", sb_)
                      mmgroup(PB[sb_][:, sh_ * 256:(sh_ + 1) * 256],
                              [(rkb[b_][:, n * 256 + jt * 128: n * 256 + (jt + 1) * 128], rqb[b_][:, csl])],
                              reads=(("rkb", b_), ("rqb", b_)), writes=(skey,))
                      pi = (c % 2) * 2 + jt
                      tt("dve", ptr2[pi][:, :], PB[sb_][:, sh_ * 256:(sh_ + 1) * 256], trim[:, jt * 256:(jt + 1) * 256],
                         ALU.mult, reads=(skey, "trim"), writes=(("ptr2", pi),))
                  yh = c % 2
                  yps = PB[4 + yh][:, 0:256]
                  ykey = ("ps", 4 + yh)
                  pairs = [(rv3[:, 2 * n + jt, :], ptr2[(c % 2) * 2 + jt][:, :]) for jt in range(2)]
                  if n > 0:
                      pairs.append((sl4[:, h, n, :], rqb[b_][:, csl]))
                  pairs.append((sinb3[:, n, :], rqb[b_][:, csl]))
                  mmgroup(yps, pairs, reads=(("rvb", b_), ("ptr2", (c % 2) * 2), ("ptr2", (c % 2) * 2 + 1),
                                             ("sloc", h), "sinb", ("rqb", b_)), writes=(ykey,))
                  act(ysq[yh][:, :], yps, AF.Square, reads=(ykey,), writes=(("ysq", yh),))
                  sps = PB[6 + yh][:, 0:256]
                  mmgroup(sps, [(ones_bf, ysq[yh][:, :])], reads=(("ysq", yh), "cm"), writes=(("ps", 6 + yh),))
                  ts("dve", rst[yh][:, :], sps, 1.0 / 128, EPS, ALU.mult, ALU.add, reads=(("ps", 6 + yh),),
                     writes=(("rst", yh),))
                  act(rst[yh][:, :], rst[yh][:, :], AF.Sqrt, reads=(("rst", yh),), writes=(("rst", yh),))
                  P.op("dve", lambda e, yh=yh: e.reciprocal(out=rst[yh][:, :], in_=rst[yh][:, :]), reads=(("rst", yh),),
                       writes=(("rst", yh),))
                  stt("dve", ytm[yh][:, :], yps, gains[:, gcol:gcol + 1], rst[yh][:, :], ALU.mult, ALU.mult,
                      reads=(ykey, ("rst", yh), "gains"), writes=(("ytm", yh),))
                  tt("dve", mix3[:, 4 + h, csl], ytm[yh][:, :], rgb[b_][:, csl], ALU.mult,
                     reads=(("ytm", yh), ("rgb", b_)), writes=(("mix", 4 + h),))

          chk("B2")
          P.barrier()
          A.o = AB
          kaug = [A.get(nm("kaug"), 4096, BF16) for _ in range(2)]
          vaug = [A.get(nm("vaug"), 32 * 128, BF16) for _ in range(2)]
          qaug = [A.get(nm("qaug"), NT, BF16) for _ in range(2)]
          ptr = [A.get(nm("ptr"), 512, BF16) for _ in range(4)]
          gaddt = A.get(nm("gaddt"), 256, F32)
          keept = A.get(nm("keept"), 256, F32)
          dmk = A.get(nm("dmk"), 4 * 256, BF16)
          gm = A.get(nm("gm"), 256, F32)
          g1 = A.get(nm("g1"), 256, F32)
          eq = A.get(nm("eq"), 256, F32)
          mx = A.get(nm("mx"), 16, F32)
          ksum = A.get(nm("ksum"), 16, F32)
          kmb = A.get(nm("kmb"), 16, BF16)
          mb = A.get(nm("mb"), 16 * 80, BF16)
          rec = [A.get(nm("rec"), 256, F32) for _ in range(2)]
          mb3 = mb[:, :].rearrange("p (q c) -> p q c", q=16)
          P.dma("sp", gaddt[:, :], gadd_d[:, :], writes=("gaddt",))
          P.dma("sp", keept[:, :], keepn_d[:, :], writes=("keept",))
          P.dma("sp", dmk[:, :], dmask_d[:, :], writes=("dmk",))
          P.op("dve", lambda en: en.memset(mb[:, :], 0.0), writes=("mb",))
          for b_ in range(2):
              P.dma("sp", kaug[b_][64:80, :], ind_d[:, :], writes=(("kaug", b_),))
              P.op("dve", lambda en, b_=b_: en.memset(vaug[b_][:, :], 1.0), writes=(("vaug", b_),))

          def moba_load(h):
              b_ = h % 2
              pp, par = h // 2, h % 2
              for r in range(2):
                  P.dma("sp", kaug[b_][0:64, r * NT:(r + 1) * NT],
                        AGK[r * 512 + pp * 128 + par * 64: r * 512 + pp * 128 + par * 64 + 64, :],
                        reads=("AGK",), writes=(("kaug", b_),))
                  P.dma("sp", vaug[b_][:, :].rearrange("p (t c) -> p t c", t=32)[:, r * 16:(r + 1) * 16,
                                                                                   par * 64:par * 64 + 64],
                        AGV[r * 1024 + h * 128: r * 1024 + (h + 1) * 128, :].rearrange("p (t c) -> p t c", t=16),
                        reads=("AGV",), writes=(("vaug", b_),))
              P.dma("sp", qaug[b_][0:64, :], QT[pp * 128 + par * 64: pp * 128 + par * 64 + 64, :],
                    reads=(), writes=(("qaug", b_),))

          def vaug_ap(b_, T, par):
              return vaug[b_][:, T * 128:(T + 1) * 128]

          def gate1(hh):
              bb = hh % 2
              kqq = ("kaug", bb)
              qqq = ("qaug", bb)
              P.op("dve", lambda en, b_=bb: en.tensor_reduce(
                  out=ksum[0:64, :], in_=kaug[b_][0:64, :].rearrange("p (n c) -> p n c", n=16), axis=AX.X, op=ALU.add),
                  reads=(kqq,), writes=("ksum",))
              P.op("dve", lambda en: en.tensor_copy(out=kmb[0:64, :], in_=ksum[0:64, :]), reads=("ksum",),
                   writes=("kmb",))
              fns = [(lambda en, qt=qt, b_=bb: en.matmul(PB[0][:, qt * 16:(qt + 1) * 16],
                                                        qaug[b_][0:64, qt * 128:(qt + 1) * 128], kmb[0:64, :],
                                                        start=True, stop=True)) for qt in range(16)]
              P.group("pe", fns, reads=(qqq, "kmb"), writes=(("ps", 0),))
              tt("dve", gm[:, :], PB[0][:, 0:256], gaddt[:, :], ALU.add, reads=(("ps", 0), "gaddt"), writes=("gm",))

              def g3(t):
                  return t[:, :].rearrange("p (q n) -> p q n", q=16)

              def mxb():
                  return mx[:, :].unsqueeze(2).to_broadcast([128, 16, 16])

              cur = gm
              ckey = "gm"
              for it in range(3):
                  P.op("dve", lambda en, cur=cur: en.tensor_reduce(out=mx[:, :], in_=g3(cur), axis=AX.X, op=ALU.max),
                       reads=(ckey,), writes=("mx",))
                  if it == 2:
                      break
                  P.op("dve", lambda en, cur=cur: en.tensor_tensor(out=g3(eq), in0=g3(cur), in1=mxb(), op=ALU.is_ge),
                       reads=(ckey, "mx"), writes=("eq",))
                  stt("dve", g1[:, :], eq[:, :], NEG, cur[:, :], ALU.mult, ALU.add, reads=("eq", ckey),
                      writes=("g1",))
                  cur = g1
                  ckey = "g1"
              ts("dve", mx[:, :], mx[:, :], -1.0e20, None, ALU.max, ALU.bypass, reads=("mx",), writes=("mx",))
              P.op("dve", lambda en: en.tensor_tensor(out=g3(eq), in0=g3(gm), in1=mxb(), op=ALU.is_lt),
                   reads=("gm", "mx"), writes=("eq",))
              P.op("dve", lambda en: en.tensor_tensor(out=mb3[:, :, 64:80], in0=g3(eq), in1=g3(keept), op=ALU.mult),
                   reads=("eq", "keept"), writes=("mb",))

          def gate2(hh):
              bb = hh % 2
              qqq = ("qaug", bb)
              for half in range(2):
                  fns = [(lambda en, j=j, half=half: en.matmul(
                      PB[(0, 2)[j // 4]][0:80, (j % 4) * 128:(j % 4 + 1) * 128], mb3[:, half * 8 + j, :], ident,
                      start=True, stop=True)) for j in range(8)]
                  P.group("pe", fns, reads=("mb", "cm"), writes=(("ps", 0), ("ps", 2)))
                  for j2 in range(2):
                      c0 = half * 1024 + j2 * 512
                      P.op("dve", lambda en, j2=j2, c0=c0, b_=bb: en.tensor_copy(
                          out=qaug[b_][64:80, c0:c0 + 512], in_=PB[(0, 2)[j2]][64:80, :]),
                          reads=(("ps", (0, 2)[j2]),), writes=(qqq,))

          moba_load(0)
          gate1(0)
          gate2(0)
          sring = [(4, 0), (5, 0), (6, 0), (7, 0)]
          for h in range(8):
              b_ = h % 2
              pp, par = h // 2, h % 2
              if h + 1 < 8:
                  moba_load(h + 1)
              kq = ("kaug", b_)
              qq = ("qaug", b_)
              for qb in range(8):
                  if h + 1 < 8 and qb == 2:
                      gate1(h + 1)
                  if h + 1 < 8 and qb == 5:
                      gate2(h + 1)
                  ob = 3 if qb % 2 == 0 else 1
                  ops_ = PB[ob][:, 0:256]
                  okey = ("ps", ob)
                  blocks = [(r, n) for r in range(2) for n in (range(8) if r == 0 else range(qb + 1))]
                  nbk = len(blocks)
                  qsl = slice(qb * 256, (qb + 1) * 256)

                  def s_mm(i):
                      r, n = blocks[i]
                      sb_ = sring[i % 4][0]
                      fns = []
                      for kt in range(2):
                          T = r * 16 + n * 2 + kt
                          fns.append(lambda en, T=T, kt=kt, sb_=sb_: en.matmul(
                              PB[sb_][:, kt * 256:(kt + 1) * 256], kaug[b_][0:80, T * 128:(T + 1) * 128],
                              qaug[b_][0:80, qsl], start=True, stop=True))
                      P.group("pe", fns, reads=(kq, qq), writes=(("ps", sb_),))

                  def pv(i):
                      r, n = blocks[i]
                      sb_ = sring[i % 4][0]
                      pi = i % 4
                      act(ptr[pi][:, :], PB[sb_][:, :], AF.Exp, reads=(("ps", sb_),), writes=(("ptr", pi),),
                          scale=0.125)
                      if n == qb:
                          tt("dve", ptr[pi][:, :], ptr[pi][:, :], dmk[:, r * 512:(r + 1) * 512], ALU.mult,
                             reads=(("ptr", pi), "dmk"), writes=(("ptr", pi),))
                      for kt in range(2):
                          T = r * 16 + n * 2 + kt
                          va = vaug_ap(b_, T, par)
                          P.op("pe", lambda en, va=va, pi=pi, i=i, kt=kt: en.matmul(
                              ops_, va, ptr[pi][:, kt * 256:(kt + 1) * 256], start=(i == 0 and kt == 0),
                              stop=(i == nbk - 1 and kt == 1)),
                              reads=(("vaug", b_), ("ptr", pi)), writes=(okey,))

                  s_mm(0)
                  if nbk > 1:
                      s_mm(1)
                  for i in range(nbk):
                      if i + 2 < nbk:
                          s_mm(i + 2)
                      pv(i)
                  nr = slice(0, 64) if par == 0 else slice(64, 128)
                  sr = slice(64, 128) if par == 0 else slice(0, 64)
                  rc = rec[qb % 2]
                  P.op("dve", lambda en, rc=rc, nr=nr, sr=sr: en.reciprocal(out=rc[nr, :], in_=ops_[sr, :]),
                       reads=(okey,), writes=(("rec", qb % 2),))
                  tt("dve", mix3[nr, pp, qsl], ops_[nr, :], rc[nr, :], ALU.mult, reads=(okey, ("rec", qb % 2)),
                     writes=(("mix", pp),))

          chk("B1")
          P.barrier()
          ri = [0]

          def resid_add(ps_ap, pkey, oc, tsl):
              tt("dve", hT3[:, oc, tsl], ps_ap, hT3[:, oc, tsl], ALU.add, reads=(pkey, ("h", oc)), writes=(("h", oc),))

          for hf in range(2):
              s = wensure(WIDX[(l, "out", hf)])
              W = wslot_view(s, 8, 512)
              for sub in range(4):
                  oc = hf * 4 + sub
                  for tg in range(4):
                      b = ri[0] % 7
                      ri[0] += 1
                      tsl = slice(tg * 512, (tg + 1) * 512)
                      mmgroup(PB[b][:, :], [(W[:, kc, sub * 128:(sub + 1) * 128], mix3[:, kc, tsl]) for kc in range(8)],
                              reads=[("mix", kc) for kc in range(8)] + [("w", s)], writes=(("ps", b),))
                      resid_add(PB[b][:, :], ("ps", b), oc, tsl)
          dump(f"hmix{l}")

          chk("C")
          P.barrier()
          A = Arena()
          nT = A.get(nm("nT"), 8 * NT, BF16)
          nT3 = nT[:, :].rearrange("p (k t) -> p k t", k=8)
          actT = A.get(nm("actT"), 22 * 1024, BF16)
          act3 = actT[:, :].rearrange("p (k t) -> p k t", k=22)
          sgt = [A.get(nm("sgt"), 512, F32) for _ in range(2)]
          rmsnorm(A, nT3, 16 + l * 8)
          si = [0]
          for th in range(2):
              for j in range(11):
                  s = wensure(WIDX[(l, "fi", th, j)])
                  W = wslot_view(s, 8, 512)
                  for sub in range(2):
                      for tgl in range(2):
                          tsl = slice(th * 1024 + tgl * 512, th * 1024 + (tgl + 1) * 512)
                          bg = ri[0] % 7
                          bu = (ri[0] + 1) % 7
                          ri[0] += 2
                          mmgroup(PB[bg][:, :], [(W[:, kc, sub * 128:(sub + 1) * 128], nT3[:, kc, tsl]) for kc in range(8)],
                                  reads=nkeys + [("w", s)], writes=(("ps", bg),))
                          mmgroup(PB[bu][:, :], [(W[:, kc, 256 + sub * 128:256 + (sub + 1) * 128], nT3[:, kc, tsl])
                                                 for kc in range(8)],
                                  reads=nkeys + [("w", s)], writes=(("ps", bu),))
                          sg = sgt[si[0] % 2]
                          sk = ("sgt", si[0] % 2)
                          si[0] += 1
                          act(sg[:, :], PB[bg][:, :], AF.Silu, reads=(("ps", bg),), writes=(sk,))
                          tt("dve", act3[:, j * 2 + sub, tgl * 512:(tgl + 1) * 512], PB[bu][:, :], sg[:, :], ALU.mult,
                             reads=(("ps", bu), sk), writes=(("actT", j * 2 + sub),))
              for oc in range(8):
                  s = wensure(WIDX[(l, "fo", th, oc)])
                  W = wslot_view(s, 22, 128)
                  for tgl in range(2):
                      tsl = slice(th * 1024 + tgl * 512, th * 1024 + (tgl + 1) * 512)
                      b = ri[0] % 7
                      ri[0] += 1
                      mmgroup(PB[b][:, :], [(W[:, kc, :], act3[:, kc, tgl * 512:(tgl + 1) * 512]) for kc in range(22)],
                              reads=[("actT", kc) for kc in range(22)] + [("w", s)], writes=(("ps", b),))
                      resid_add(PB[b][:, :], ("ps", b), oc, tsl)
          dump(f"hffn{l}")

          chk("FFN")
          P.barrier()
          A = Arena()
          nT = A.get(nm("nT"), 8 * NT, BF16)
          nT3 = nT[:, :].rearrange("p (k t) -> p k t", k=8)
          pTs = A.get(nm("pTs"), 2 * NT, BF16)
          pT3 = pTs[:, :].rearrange("p (k t) -> p k t", k=2)
          sgt = [A.get(nm("sgp"), 512, F32) for _ in range(2)]
          tpt = [A.get(nm("tpt"), 512, F32) for _ in range(2)]
          wpt = A.get(nm("wpt"), 2 * 1024, BF16)
          Wp = wpt[:, :].rearrange("p (k c) -> p k c", k=2)
          P.dma("pool", Wp, w_pp[l, :, :].rearrange("(k p) c -> p k c", p=128), writes=("wpt",))
          P.dma("pool", pT3, pT_d[l, :, :].rearrange("(k p) t -> p k t", p=128), writes=("pTs",))
          rmsnorm(A, nT3, 32 + l * 8)
          for hf in range(2):
              s = wensure(WIDX[(l, "pg", hf)])
              W = wslot_view(s, 8, 512)
              for sub in range(4):
                  oc = hf * 4 + sub
                  for tg in range(4):
                      tsl = slice(tg * 512, (tg + 1) * 512)
                      bg = ri[0] % 7
                      bu = (ri[0] + 1) % 7
                      ri[0] += 2
                      mmgroup(PB[bg][:, :], [(W[:, kc, sub * 128:(sub + 1) * 128], nT3[:, kc, tsl]) for kc in range(8)],
                              reads=nkeys + [("w", s)], writes=(("ps", bg),))
                      mmgroup(PB[bu][:, :], [(Wp[:, k2, oc * 128:(oc + 1) * 128], pT3[:, k2, tsl]) for k2 in range(2)],
                              reads=("pTs", "wpt"), writes=(("ps", bu),))
                      i2 = si[0] % 2
                      si[0] += 1
                      act(sgt[i2][:, :], PB[bg][:, :], AF.Sigmoid, reads=(("ps", bg),), writes=(("sgp", i2),))
                      tt("dve", tpt[i2][:, :], PB[bu][:, :], sgt[i2][:, :], ALU.mult, reads=(("ps", bu), ("sgp", i2)),
                         writes=(("tpt", i2),))
                      tt("dve", hT3[:, oc, tsl], tpt[i2][:, :], hT3[:, oc, tsl], ALU.add,
                         reads=(("tpt", i2), ("h", oc)), writes=(("h", oc),))
          dump(f"hple{l}")

    except _Stop:
        pass

    P.barrier()
    A = Arena()
    if last:
        rmsnorm(A, None, 48, out_f32_dram=outT)
    else:
        for kc in range(8):
            P.dma("sp", outT[kc * 128:(kc + 1) * 128, :], hT3[:, kc, :], reads=(("h", kc),))
    P.final_wait("sp")

    semnames = P.semnames + ["pool_cc"]
    import contextlib
    with contextlib.ExitStack() as st:
        sems = {n: st.enter_context(nc.semaphore(n)) for n in semnames}
        block = st.enter_context(nc.Block())

        @block.tensor
        def _(e):
            P.replay("pe", e, sems)

        @block.scalar
        def _(e):
            P.replay("act", e, sems)

        @block.vector
        def _(e):
            P.replay("dve", e, sems)

        @block.gpsimd
        def _(e):
            P.replay("pool", e, sems)

        @block.sync
        def _(e):
            P.replay("sp", e, sems)
    return nc


def _tables(half):
    f32 = np.float32
    pos = (half * NT + np.arange(NT)).astype(f32)
    inv = (1.0 / (f32(10000.0) ** np.linspace(0.0, 1.0, 64, dtype=f32))).astype(f32)
    ang = (pos[:, None] * inv[None, :]).astype(f32)
    cosT = np.ascontiguousarray(np.cos(ang).astype(f32).T[np.arange(128) // 2])
    sinT = np.ascontiguousarray(np.sin(ang).astype(f32).T[np.arange(128) // 2])
    logg = np.log1p(-np.exp2(-5.0 - np.arange(4, dtype=np.float64)))
    i = np.arange(256, dtype=np.float64)
    dec = np.zeros((8, 256), np.float64)
    for h in range(4):
        dec[h] = np.exp((i + 1.0) * logg[h])
        dec[4 + h] = np.exp(-(i + 1.0) * logg[h]) * (128.0 ** -0.5)
    dectab = np.ascontiguousarray(np.broadcast_to(dec.reshape(1, -1), (128, 2048))).astype(f32)
    gadd = np.zeros((16, 16), f32)
    keepn = np.full((16, 16), MBIG, f32)
    for qt in range(16):
        bg = half * 8 + qt // 2
        gadd[qt, bg:] = NEG
        keepn[qt, bg] = 0.0
    gadd = np.ascontiguousarray(np.broadcast_to(gadd.reshape(1, -1), (128, 256)))
    keepn = np.ascontiguousarray(np.broadcast_to(keepn.reshape(1, -1), (128, 256)))
    k = np.arange(128)[:, None]
    q = np.arange(256)[None, :]
    tri = [(kt * 128 + k <= q).astype(f32) for kt in range(2)]
    trimask = np.concatenate(tri, axis=1)
    dm = []
    for r in range(2):
        for kt in range(2):
            if r == half:
                dm.append(tri[kt])
            elif r < half:
                dm.append(np.ones((128, 256), f32))
            else:
                dm.append(np.zeros((128, 256), f32))
    dmask = np.concatenate(dm, axis=1).astype(ml_dtypes.bfloat16)
    ind = np.zeros((16, 4096), f32)
    for n in range(16):
        ind[n, n * 256:(n + 1) * 256] = 1.0
    ident = np.eye(128, dtype=f32)
    ones = np.ones((128, 128), f32)
    jm = np.zeros((128, 128), f32)
    for a in range(64):
        jm[2 * a + 1, 2 * a] = -1.0
        jm[2 * a, 2 * a + 1] = 1.0
    cm = np.concatenate([ident, ones, jm], axis=1).astype(ml_dtypes.bfloat16)
    flag = np.full((128, 1), float(half), f32)
    return dict(cosT=cosT, sinT=sinT, dectab=dectab, gadd=gadd, keepneg=keepn, dmask=dmask, trimask=trimask,
                indrows=ind.astype(ml_dtypes.bfloat16), cmats=cm, flag=flag)


def _gains(attn_norm_g, ffn_norm_g, ple_norm_g, final_norm_g, ret_norm_g):
    g = np.zeros((128, 64), np.float32)
    for l in range(DEPTH):
        g[:, l * 8:(l + 1) * 8] = attn_norm_g[l].reshape(8, 128).T
        g[:, 16 + l * 8:16 + (l + 1) * 8] = ffn_norm_g[l].reshape(8, 128).T
        g[:, 32 + l * 8:32 + (l + 1) * 8] = ple_norm_g[l].reshape(8, 128).T
        g[:, 56 + l * 4:56 + (l + 1) * 4] = ret_norm_g[l].reshape(4, 128).T
    g[:, 48:56] = final_norm_g.reshape(8, 128).T
    return g


_CACHE = {}


def _run(layers, first, last, xTs, shared, dbg=()):
    key = (tuple(layers), first, last, tuple(dbg))
    if key not in _CACHE:
        _CACHE[key] = build_program(list(layers), first, last, dbg)
    nc = _CACHE[key]
    in_maps = []
    for c in range(8):
        m = dict(shared)
        m.update(_tables(c % 2))
        m["xT"] = xTs[c]
        m["pT"] = shared["pT_all"][c]
        del m["pT_all"]
        in_maps.append(m)
    ncores = int(os.environ.get("KCORES", "8"))
    res = run_bass_kernel_spmd(nc, in_maps[:ncores], core_ids=list(range(ncores)))
    rr = list(res.results)
    while len(rr) < 8:
        rr.append(rr[0])
    return rr


def kernel(x, p, attn_norm_g, w_in, ret_norm_g, w_out, ffn_norm_g, w_ffn_in, w_ffn_out, ple_norm_g,
           w_ple_gate, w_ple_proj, final_norm_g, _dbg=(), _split=False):
    f32 = np.float32
    x = np.asarray(x, f32)
    p = np.asarray(p, f32)
    xTs, pTs = [], []
    for c in range(8):
        b, hf = c // 2, c % 2
        xTs.append(np.ascontiguousarray(x[b, hf * NT:(hf + 1) * NT, :].T))
        pTs.append(np.ascontiguousarray(p[:, b, hf * NT:(hf + 1) * NT, :].transpose(0, 2, 1)))
    shared = dict(
        w_in=np.asarray(w_in, f32), w_out=np.asarray(w_out, f32), w_ffn_in=np.asarray(w_ffn_in, f32),
        w_ffn_out=np.asarray(w_ffn_out, f32), w_ple_gate=np.asarray(w_ple_gate, f32),
        w_ple_proj=np.asarray(w_ple_proj, f32),
        gains=_gains(np.asarray(attn_norm_g, f32), np.asarray(ffn_norm_g, f32), np.asarray(ple_norm_g, f32),
                     np.asarray(final_norm_g, f32), np.asarray(ret_norm_g, f32)),
        pT_all=pTs,
    )
    if _split:
        r0 = _run([0], True, False, xTs, shared)
        xT1 = [np.ascontiguousarray(r["outT"]) for r in r0]
        res = _run([1], False, True, xT1, shared)
    else:
        res = _run([0, 1], True, True, xTs, shared, dbg=_dbg)
    out = np.zeros((4, 4096, D), f32)
    for c in range(8):
        b, hf = c // 2, c % 2
        out[b, hf * NT:(hf + 1) * NT, :] = res[c]["outT"].T
    if _dbg:
        return out, res
    return out
```

```python
import os
import numpy as np
import ml_dtypes
import concourse.bass as bass
import concourse.mybir as mybir
from concourse.bass_utils import run_bass_kernel_spmd

F32 = mybir.dt.float32
BF16 = mybir.dt.bfloat16
ALU = mybir.AluOpType
AF = mybir.ActivationFunctionType
AX = mybir.AxisListType

D = 1024
NT = 2048
DEPTH = 2
DFF = 2816
NEG = -1.0e30
MBIG = -30000.0
EPS = 1e-6
RG_PAIRS = [[0, 1], [2, 3], [4, 5], [6, 7]]
ENGS = ("pe", "act", "dve", "pool", "sp")


class _Rec:
    def __getattr__(self, name):
        def f(*a, **k):
            self.call = (name, a, k)
            return self
        return f


class Prog:
    def __init__(self):
        self.ops = {e: [] for e in ENGS}
        self.cnt = {e: 0 for e in ENGS}
        self.waited = {e: {} for e in ENGS}
        self.lastw = {}
        self.readers = {}
        self.dcnt = {}
        self.dma_ring = [f"dg{i}" for i in range(24)]
        self.dma_ring_i = 0
        self.semnames = [e for e in ENGS if e != "sp"] + self.dma_ring + [f"w{i}" for i in range(4)]

    def _need(self, eng, tok):
        sem, val = tok
        if self.waited[eng].get(sem, 0) < val:
            self.waited[eng][sem] = val
            self.ops[eng].append(("wait", sem, val))

    def _deps(self, eng, reads, writes):
        for k in reads:
            t = self.lastw.get(k)
            if t is not None and not (t[0] == eng and eng == "pe"):
                self._need(eng, t)
            if isinstance(k, tuple) and k[0] == "ps":
                for s, v in self.readers.get(k, {}).items():
                    if s != eng:
                        self._need(eng, (s, v))
        for k in writes:
            t = self.lastw.get(k)
            if t is not None and t[0] != eng:
                self._need(eng, t)
            for s, v in self.readers.get(k, {}).items():
                if s != eng:
                    self._need(eng, (s, v))

    def _commit(self, tok, reads, writes):
        for k in reads:
            d = self.readers.setdefault(k, {})
            if d.get(tok[0], 0) < tok[1]:
                d[tok[0]] = tok[1]
        for k in writes:
            self.lastw[k] = tok
            self.readers[k] = {}

    def op(self, eng, fn, reads=(), writes=()):
        self.group(eng, [fn], reads, writes)

    def group(self, eng, fns, reads=(), writes=()):
        self._deps(eng, reads, writes)
        for i, fn in enumerate(fns):
            rec = _Rec()
            fn(rec)
            self.ops[eng].append(("op", rec.call, i == len(fns) - 1))
        self.cnt[eng] += 1
        self._commit((eng, self.cnt[eng]), reads, writes)

    def dma(self, q, out, in_, reads=(), writes=(), sem=None):
        if sem is None:
            sem = self.dma_ring[self.dma_ring_i % len(self.dma_ring)]
            self.dma_ring_i += 1
            prev = self.dcnt.get(sem, 0)
            if prev:
                self._need(q, (sem, prev))
        self._deps(q, reads, writes)
        self.dcnt[sem] = self.dcnt.get(sem, 0) + 16
        self.ops[q].append(("dma", out, in_, sem))
        self._commit((sem, self.dcnt[sem]), reads, writes)

    def collective(self, ins, outs, reads, writes, sem):
        q = "pool"
        self._deps(q, reads, writes)
        self.dcnt[sem] = self.dcnt.get(sem, 0) + 1
        self.ops[q].append(("cc", ins, outs, sem))
        self._commit((sem, self.dcnt[sem]), reads, writes)

    def barrier(self):
        toks = [(e, self.cnt[e]) for e in ENGS if e != "sp" and self.cnt[e]]
        toks += [(s, v) for s, v in self.dcnt.items() if not s.startswith("w") and s != "pool_cc"]
        for e in ENGS:
            for t in toks:
                self._need(e, t)

    def final_wait(self, eng="sp"):
        for s, v in self.dcnt.items():
            self._need(eng, (s, v))
        for e in ENGS:
            if e != "sp" and self.cnt[e]:
                self._need(eng, (e, self.cnt[e]))

    def replay(self, eng, e, sems):
        pend = []

        def flush(keep=0):
            while len(pend) > keep:
                s_, v_ = pend.pop(0)
                e.wait_ge(sems[s_], v_)

        for o in self.ops[eng]:
            if o[0] == "wait":
                pend.append((o[1], o[2]))
                continue
            if o[0] == "op":
                flush(keep=1)
                name, a, k = o[1]
                ins = getattr(e, name)(*a, **k)
                if pend:
                    s_, v_ = pend.pop(0)
                    ins._wait_ge(sems[s_], v_)
                if o[2]:
                    ins.then_inc(sems[eng], 1)
                continue
            flush()
            if False:
                pass
            elif o[0] == "dma":
                e.dma_start(out=o[1], in_=o[2]).then_inc(sems[o[3]], 16)
            elif o[0] == "cc":
                e.collective_compute("AllGather", ALU.bypass, replica_groups=RG_PAIRS,
                                     ins=[o[1]], outs=[o[2]]).then_inc(sems[o[3]])
        flush()


def build_program(layers, first, last, dbg=()):
    nc = bass.Bass("TRN2", target_bir_lowering=False)
    P = Prog()

    def ext(name, shape, dt=F32, out=False):
        return nc.dram_tensor(name, list(shape), dt, kind="ExternalOutput" if out else "ExternalInput").ap()

    xT = ext("xT", [D, NT])
    pT_d = ext("pT", [DEPTH, 256, NT])
    w_in = ext("w_in", [DEPTH, D, 3584])
    w_out = ext("w_out", [DEPTH, D, D])
    w_fi = ext("w_ffn_in", [DEPTH, D, 2 * DFF])
    w_fo = ext("w_ffn_out", [DEPTH, DFF, D])
    w_pg = ext("w_ple_gate", [DEPTH, D, D])
    w_pp = ext("w_ple_proj", [DEPTH, 256, D])
    gains_d = ext("gains", [128, 64])
    cos_d = ext("cosT", [128, NT])
    sin_d = ext("sinT", [128, NT])
    dec_d = ext("dectab", [128, 8 * 256])
    gadd_d = ext("gadd", [128, 256])
    keepn_d = ext("keepneg", [128, 256])
    dmask_d = ext("dmask", [128, 4 * 256], BF16)
    tri_d = ext("trimask", [128, 2 * 256])
    ind_d = ext("indrows", [16, 4096], BF16)
    cmat_d = ext("cmats", [128, 3 * 128], BF16)
    flag_d = ext("flag", [128, 1])
    outT = ext("outT", [D, NT], out=True)
    dbg_d = {n: ext("dbg_" + n, [D, NT], out=True) for n in dbg}

    def scr(name, shape, dt):
        return nc.dram_tensor(name, list(shape), dt).ap()

    QT = scr("scrQT", [512, NT], BF16)
    EK = scr("expK", [512, NT], BF16)
    EV = scr("expV", [1024, 1024], BF16)
    ES = scr("expS", [512, 128], F32)
    AGK = scr("agK", [1024, NT], BF16)
    AGV = scr("agV", [2048, 1024], BF16)
    AGS = scr("agS", [1024, 128], F32)
    RQ = scr("scrRQ", [512, NT], BF16)
    RK = scr("scrRK", [512, NT], BF16)
    RG = scr("scrRG", [512, NT], BF16)
    VR = scr("scrVR", [128, 16 * 512], BF16)

    off = [16512]

    def sb(name, cols, dt, at=None):
        nbytes = cols * (4 if dt == F32 else 2)
        if at is None:
            o = off[0]
            off[0] += (nbytes + 31) // 32 * 32
        else:
            o = at
        return nc.alloc_sbuf_tensor_at(name, [128, cols], dt, offset=o)

    hT = sb("hT", 8 * NT, F32)
    wsl = [sb(f"wsl{i}", 4096, BF16) for i in range(4)]
    cm = sb("cmats", 384, BF16)
    gains = sb("gains", 64, F32)
    flag = sb("flag", 1, F32)
    sloc = sb("sloc", 4 * 8 * 128, BF16)
    ARENA = off[0]
    ident = cm[:, 0:128]
    ones_bf = cm[:, 128:256]
    jmat = cm[:, 256:384]

    class Arena:
        def __init__(self):
            self.o = ARENA

        def get(self, name, cols, dt):
            nbytes = cols * (4 if dt == F32 else 2)
            t = sb(name, cols, dt, at=self.o)
            self.o += (nbytes + 31) // 32 * 32
            assert self.o <= 229344, (name, self.o)
            return t

    uid = [0]

    def nm(s):
        uid[0] += 1
        return f"{s}_{uid[0]}"

    PB = [nc.alloc_psum_tensor(f"pb{i}", [128, 512], F32) for i in range(8)]
    PT = PB[7]

    def act(out, in_, func, reads, writes, scale=1.0, bias=0.0):
        P.op("act", lambda e: e.activation(out=out, in_=in_, func=func, bias=bias, scale=scale), reads, writes)

    def tt(eng, out, in0, in1, op, reads, writes):
        P.op(eng, lambda e: e.tensor_tensor(out=out, in0=in0, in1=in1, op=op), reads, writes)

    def ts(eng, out, in0, s1, s2, op0, op1, reads, writes):
        if s2 is None:
            P.op(eng, lambda e: e.tensor_scalar(out=out, in0=in0, scalar1=s1, scalar2=None, op0=op0), reads, writes)
        else:
            P.op(eng, lambda e: e.tensor_scalar(out=out, in0=in0, scalar1=s1, scalar2=s2, op0=op0, op1=op1),
                 reads, writes)

    def stt(eng, out, in0, scalar, in1, op0, op1, reads, writes):
        P.op(eng, lambda e: e.scalar_tensor_tensor(out=out, in0=in0, scalar=scalar, in1=in1, op0=op0, op1=op1),
             reads, writes)

    def mmgroup(out, pairs, reads, writes):
        n = len(pairs)
        fns = []
        for i, (l, r) in enumerate(pairs):
            fns.append(lambda e, l=l, r=r, i=i: e.matmul(out, l, r, start=(i == 0), stop=(i == n - 1)))
        P.group("pe", fns, reads, writes)

    wloads = []
    wissued = [0]

    def wslot_view(s, k, c):
        return wsl[s][:, 0:k * c].rearrange("p (k c) -> p k c", k=k)

    def wensure(i):
        while wissued[0] < min(i + 4, len(wloads)):
            j = wissued[0]
            s = j % 4
            for (ofn, in_ap) in wloads[j]:
                P.dma("pool", ofn(s), in_ap, reads=(), writes=(("w", s),), sem=f"w{s}")
            wissued[0] += 1
        return i % 4

    def wreg(parts):
        wloads.append(parts)
        return len(wloads) - 1

    WIDX = {}
    for l in layers:
        for b, c0 in enumerate([0, 512, 1024, 2560, 1536, 2048, 3072]):
            WIDX[(l, "in", b)] = wreg([(lambda s: wslot_view(s, 8, 512),
                                        w_in[l, :, c0:c0 + 512].rearrange("(k p) c -> p k c", p=128))])
        for hf in range(2):
            WIDX[(l, "out", hf)] = wreg([(lambda s: wslot_view(s, 8, 512),
                                          w_out[l, :, hf * 512:(hf + 1) * 512].rearrange("(k p) c -> p k c", p=128))])
        for th in range(2):
            for j in range(11):
                WIDX[(l, "fi", th, j)] = wreg([
                    (lambda s: wslot_view(s, 8, 512)[:, :, 0:256],
                     w_fi[l, :, j * 256:(j + 1) * 256].rearrange("(k p) c -> p k c", p=128)),
                    (lambda s: wslot_view(s, 8, 512)[:, :, 256:512],
                     w_fi[l, :, DFF + j * 256:DFF + (j + 1) * 256].rearrange("(k p) c -> p k c", p=128))])
            for oc in range(8):
                WIDX[(l, "fo", th, oc)] = wreg([(lambda s: wslot_view(s, 22, 128),
                                                 w_fo[l, :, oc * 128:(oc + 1) * 128].rearrange("(k p) c -> p k c", p=128))])
        for hf in range(2):
            WIDX[(l, "pg", hf)] = wreg([(lambda s: wslot_view(s, 8, 512),
                                         w_pg[l, :, hf * 512:(hf + 1) * 512].rearrange("(k p) c -> p k c", p=128))])

    P.dma("sp", cm[:, :], cmat_d[:, :], writes=("cm",))
    P.dma("sp", gains[:, :], gains_d[:, :], writes=("gains",))
    P.dma("sp", flag[:, :], flag_d[:, :], writes=("flag",))
    hT3 = hT[:, :].rearrange("p (k t) -> p k t", k=8)
    for kc in range(8):
        P.dma("sp", hT3[:, kc, :], xT[kc * 128:(kc + 1) * 128, :], writes=(("h", kc),))

    def dump(name):
        if name in dbg_d:
            for kc in range(8):
                P.dma("sp", dbg_d[name][kc * 128:(kc + 1) * 128, :], hT3[:, kc, :], reads=(("h", kc),))

    def rmsnorm(A, nT3, gcol, out_f32_dram=None):
        sq = A.get(nm("sq"), 8 * 512, BF16)
        sq3 = sq[:, :].rearrange("p (k t) -> p k t", k=8)
        rs = [A.get(nm("rs"), 512, F32) for _ in range(2)]
        ob = [A.get(nm("ob"), 512, F32) for _ in range(2)] if out_f32_dram is not None else None
        for tg in range(4):
            tsl = slice(tg * 512, (tg + 1) * 512)
            for kc in range(8):
                act(sq3[:, kc, :], hT3[:, kc, tsl], AF.Square, reads=(("h", kc),), writes=(("sq", kc),))
            ps = PB[tg % 2]
            mmgroup(ps[:, :], [(ones_bf, sq3[:, kc, :]) for kc in range(8)],
                    reads=[("sq", kc) for kc in range(8)] + ["cm"], writes=(("ps", tg % 2),))
            r = rs[tg % 2]
            ts("dve", r[:, :], ps[:, :], 1.0 / D, EPS, ALU.mult, ALU.add, reads=(("ps", tg % 2),), writes=(("rs", tg % 2),))
            act(r[:, :], r[:, :], AF.Sqrt, reads=(("rs", tg % 2),), writes=(("rs", tg % 2),))
            P.op("dve", lambda e, r=r: e.reciprocal(out=r[:, :], in_=r[:, :]), reads=(("rs", tg % 2),),
                 writes=(("rs", tg % 2),))
            for kc in range(8):
                g = gains[:, gcol + kc:gcol + kc + 1]
                if out_f32_dram is None:
                    stt("dve", nT3[:, kc, tsl], hT3[:, kc, tsl], g, r[:, :], ALU.mult, ALU.mult,
                        reads=(("h", kc), ("rs", tg % 2), "gains"), writes=(("n", kc),))
                else:
                    o = ob[kc % 2]
                    stt("dve", o[:, :], hT3[:, kc, tsl], g, r[:, :], ALU.mult, ALU.mult,
                        reads=(("h", kc), ("rs", tg % 2), "gains"), writes=(("ob", kc % 2),))
                    P.dma("sp", out_f32_dram[kc * 128:(kc + 1) * 128, tsl], o[:, :], reads=(("ob", kc % 2),),
                          writes=())

    class _Stop(Exception):
        pass

    STOP = os.environ.get("KSTOP", "")

    def chk(name):
        if STOP == name:
            raise _Stop()

    try:
      for l in layers:
          li = layers.index(l)
          chk("load")
          P.barrier()
          A = Arena()
          nT = A.get(nm("nT"), 8 * NT, BF16)
          nT3 = nT[:, :].rearrange("p (k t) -> p k t", k=8)
          tabc = [A.get(nm("tabc"), 512, F32) for _ in range(2)]
          tabs = [A.get(nm("tabs"), 512, F32) for _ in range(2)]
          dect = A.get(nm("dect"), 8 * 256, F32)
          vbuf = A.get(nm("vbuf"), 16 * 512, BF16)
          evr = [A.get(nm("evr"), 512, BF16) for _ in range(4)]
          kdt = A.get(nm("kdt"), 16 * 128, BF16)
          xb = [A.get(nm("xb"), 512, BF16) for _ in range(2)]
          t1 = [A.get(nm("t1"), 512, F32) for _ in range(2)]
          t2 = [A.get(nm("t2"), 512, F32) for _ in range(2)]
          sst = A.get(nm("sst"), 128, F32)
          P.dma("sp", dect[:, :], dec_d[:, :], writes=("dect",))
          rmsnorm(A, nT3, 0 + l * 8)
          chk("norm")
          nkeys = [("n", kc) for kc in range(8)]

          evi = [0]
          pbi = [0]
          tabi = [0]

          def proj_fm(widx, sub, tg):
              s = wensure(widx)
              W = wslot_view(s, 8, 512)
              b = pbi[0] % 4
              pbi[0] += 1
              mmgroup(PB[b][:, :], [(W[:, kc, sub * 128:(sub + 1) * 128], nT3[:, kc, tg * 512:(tg + 1) * 512])
                                    for kc in range(8)],
                      reads=nkeys + [("w", s)], writes=(("ps", b),))
              return b

          def ev_out(b, dram_ap, func=AF.Copy):
              e = evi[0] % 4
              evi[0] += 1
              act(evr[e][:, :], PB[b][:, :], func, reads=(("ps", b),), writes=(("evr", e),))
              P.dma("sp", dram_ap, evr[e][:, :], reads=(("evr", e),), writes=())

          for blk, dst in ((0, QT), (1, EK)):
              for sub in range(4):
                  for tg in range(4):
                      b = proj_fm(WIDX[(l, "in", blk)], sub, tg)
                      ev_out(b, dst[sub * 128:(sub + 1) * 128, tg * 512:(tg + 1) * 512])
          chk("A1")
          vb4 = vbuf[:, :].rearrange("p (h t c) -> p h t c", h=8, t=16)
          vr3 = vbuf[:, :].rearrange("p (t c) -> p t c", t=16)
          for blk in (2, 3):
              s = wensure(WIDX[(l, "in", blk)])
              W = wslot_view(s, 8, 512)
              for t in range(16):
                  b = pbi[0] % 4
                  pbi[0] += 1
                  mmgroup(PB[b][:, :], [(nT3[:, kc, t * 128:(t + 1) * 128], W[:, kc, :]) for kc in range(8)],
                          reads=nkeys + [("w", s)], writes=(("ps", b),))
                  if blk == 2:
                      P.op("act", lambda e, b=b, t=t: e.activation(
                          out=vb4[:, :, t, :], in_=PB[b][:, :].rearrange("p (h c) -> p h c", h=8), func=AF.Copy),
                          reads=(("ps", b),), writes=("vbuf",))
                  else:
                      P.op("act", lambda e, b=b, t=t: e.activation(out=vr3[:, t, :], in_=PB[b][:, :], func=AF.Copy),
                           reads=(("ps", b),), writes=("vbuf",))
              if blk == 2:
                  P.dma("sp", EV.rearrange("(h p) c -> p h c", p=128),
                        vbuf[:, :].rearrange("p (h c) -> p h c", h=8), reads=("vbuf",), writes=("EV",))
              else:
                  P.dma("sp", VR[:, :], vbuf[:, :], reads=("vbuf",), writes=("VR",))

          chk("A2")
          def rotary(b, tg, decsl, dram_ap, keep_sb=None):
              i = tabi[0] % 2
              tabi[0] += 1
              tsl = slice(tg * 512, (tg + 1) * 512)
              P.dma("sp", tabc[i][:, :], cos_d[:, tsl], writes=(("tabc", i),))
              P.dma("sp", tabs[i][:, :], sin_d[:, tsl], writes=(("tabs", i),))
              act(xb[i][:, :], PB[b][:, :], AF.Copy, reads=(("ps", b),), writes=(("xb", i),))
              jb = 4 + i
              mmgroup(PB[jb][:, :], [(jmat, xb[i][:, :])], reads=(("xb", i), "cm"), writes=(("ps", jb),))
              tt("dve", t1[i][:, :], PB[b][:, :], tabc[i][:, :], ALU.mult, reads=(("ps", b), ("tabc", i)),
                 writes=(("t1", i),))
              tt("dve", t2[i][:, :], PB[jb][:, :], tabs[i][:, :], ALU.mult, reads=(("ps", jb), ("tabs", i)),
                 writes=(("t2", i),))
              tt("dve", t1[i][:, :], t1[i][:, :], t2[i][:, :], ALU.add, reads=(("t1", i), ("t2", i)),
                 writes=(("t1", i),))
              e = evi[0] % 4
              evi[0] += 1
              dec_b = dect[:, decsl].unsqueeze(1).to_broadcast([128, 2, 256])
              P.op("dve", lambda en: en.tensor_tensor(out=evr[e][:, :].rearrange("p (a c) -> p a c", a=2),
                                                       in0=t1[i][:, :].rearrange("p (a c) -> p a c", a=2),
                                                       in1=dec_b, op=ALU.mult),
                   reads=(("t1", i), "dect"), writes=(("evr", e),))
              P.dma("sp", dram_ap, evr[e][:, :], reads=(("evr", e),), writes=())
              return e

          for h in range(4):
              for tg in range(4):
                  b = proj_fm(WIDX[(l, "in", 4)], h, tg)
                  rotary(b, tg, slice(h * 256, (h + 1) * 256), RQ[h * 128:(h + 1) * 128, tg * 512:(tg + 1) * 512])
          chk("A3")
          kd3 = kdt[:, :].rearrange("p (t c) -> p t c", t=16)
          sl4 = sloc[:, :].rearrange("p (h n c) -> p h n c", h=4, n=8)
          gC = [float(np.exp(256.0 * np.log1p(-np.exp2(-5.0 - h)))) for h in range(4)]
          for h in range(4):
              for tg in range(4):
                  b = proj_fm(WIDX[(l, "in", 5)], h, tg)
                  e = rotary(b, tg, slice((4 + h) * 256, (5 + h) * 256),
                             RK[h * 128:(h + 1) * 128, tg * 512:(tg + 1) * 512])
                  if STOP == "A4a":
                      continue
                  fns = []
                  for a in range(4):
                      fns.append(lambda en, a=a, e=e: en.matmul(PT[:, a * 128:(a + 1) * 128],
                                                                evr[e][:, a * 128:(a + 1) * 128], ident,
                                                                start=True, stop=True))
                  P.group("pe", fns, reads=(("evr", e), "cm"), writes=(("ps", 7),))
                  P.op("dve", lambda en, tg=tg: en.tensor_copy(
                      out=kd3[:, tg * 4:(tg + 1) * 4, :], in_=PT[:, 0:512].rearrange("p (a c) -> p a c", a=4)),
                      reads=(("ps", 7),), writes=("kdt",))
              if STOP in ("A4a", "A4b"):
                  continue
              P.op("dve", lambda en: en.memset(sst[:, :], 0.0), reads=(), writes=("sst",))
              for n in range(8):
                  if n > 0:
                      P.op("dve", lambda en, n=n, h=h: en.tensor_copy(out=sl4[:, h, n, :], in_=sst[:, :]),
                           reads=("sst",), writes=(("sloc", h),))
                  mmgroup(PB[6][:, 0:128], [(kd3[:, 2 * n + j, :], vr3[:, 2 * n + j, h * 128:(h + 1) * 128])
                                            for j in range(2)],
                          reads=("kdt", "vbuf"), writes=(("ps", 6),))
                  ts("dve", sst[:, :], sst[:, :], gC[h], None, ALU.mult, ALU.bypass, reads=("sst",), writes=("sst",))
                  stt("dve", sst[:, :], PB[6][:, 0:128], gC[h], sst[:, :], ALU.mult, ALU.add,
                      reads=(("ps", 6), "sst"), writes=("sst",))
              P.dma("sp", ES[h * 128:(h + 1) * 128, :], sst[:, :], reads=("sst",), writes=("ES",))
          chk("A4")
          chk("A4a")
          chk("A4b")
          for h in range(4):
              for tg in range(4):
                  b = proj_fm(WIDX[(l, "in", 6)], h, tg)
                  ev_out(b, RG[h * 128:(h + 1) * 128, tg * 512:(tg + 1) * 512], AF.Silu)

          chk("A")
          P.barrier()
          for k_, (src, dst, key_) in enumerate(((ES, AGS, "AGS"), (EK, AGK, "AGK"), (EV, AGV, "AGV"))):
              P.collective(src.opt(), dst.opt(), reads=(), writes=(key_,), sem="pool_cc")

          chk("X")
          P.barrier()
          A = Arena()
          mixT = A.get(nm("mixT"), 8 * NT, BF16)
          mix3 = mixT[:, :].rearrange("p (k t) -> p k t", k=8)
          AB = A.o
          A.o = AB
          rqb = [A.get(nm("rqb"), NT, BF16) for _ in range(2)]
          rkb = [A.get(nm("rkb"), NT, BF16) for _ in range(2)]
          rvb = [A.get(nm("rvb"), NT, BF16) for _ in range(2)]
          rgb = [A.get(nm("rgb"), NT, BF16) for _ in range(2)]
          trim = A.get(nm("trim"), 512, F32)
          sinf = A.get(nm("sinf"), 128, F32)
          sinb = A.get(nm("sinb"), 8 * 128, BF16)
          ptr2 = [A.get(nm("ptr2"), 256, BF16) for _ in range(4)]
          ysq = [A.get(nm("ysq"), 256, BF16) for _ in range(2)]
          rst = [A.get(nm("rst"), 256, F32) for _ in range(2)]
          ytm = [A.get(nm("ytm"), 256, F32) for _ in range(2)]
          P.dma("sp", trim[:, :], tri_d[:, :], writes=("trim",))
          sinb3 = sinb[:, :].rearrange("p (n c) -> p n c", n=8)

          def ret_load(h):
              b_ = h % 2
              P.dma("sp", rqb[b_][:, :], RQ[h * 128:(h + 1) * 128, :], writes=(("rqb", b_),))
              P.dma("sp", rkb[b_][:, :], RK[h * 128:(h + 1) * 128, :], writes=(("rkb", b_),))
              P.dma("sp", rgb[b_][:, :], RG[h * 128:(h + 1) * 128, :], writes=(("rgb", b_),))
              P.dma("sp", rvb[b_][:, :].rearrange("p (t c) -> p t c", t=16),
                    VR.rearrange("p (t c) -> p t c", t=16)[:, :, h * 128:(h + 1) * 128], reads=("VR",),
                    writes=(("rvb", b_),))

          ret_load(0)
          ci = [0]
          for h in range(4):
              b_ = h % 2
              if h + 1 < 4:
                  ret_load(h + 1)
              rv3 = rvb[b_][:, :].rearrange("p (t c) -> p t c", t=16)
              P.dma("sp", sinf[:, :], AGS[h * 128:(h + 1) * 128, :], reads=("AGS",), writes=("sinf",))
              tt("dve", sinf[:, :], sinf[:, :], flag[:, 0:1].to_broadcast([128, 128]), ALU.mult,
                 reads=("sinf", "flag"), writes=("sinf",))
              for n in range(8):
                  ts("dve", sinb3[:, n, :], sinf[:, :], float(gC[h] ** n), None, ALU.mult, ALU.bypass,
                     reads=("sinf",), writes=("sinb",))
              gcol = 56 + l * 4 + h
              def ret_s1(n, c):
                  csl = slice(n * 256, (n + 1) * 256)
                  for jt in range(2):
                      sb_, sh_ = (c % 2) * 2 + jt, 0
                      skey = ("ps", sb_)
                      mmgroup(PB[sb_][:, sh_ * 256:(sh_ + 1) * 256],
                              [(rkb[b_][:, n * 256 + jt * 128: n * 256 + (jt + 1) * 128], rqb[b_][:, csl])],
                              reads=(("rkb", b_), ("rqb", b_)), writes=(skey,))
                      pi = (c % 2) * 2 + jt
                      tt("dve", ptr2[pi][:, :], PB[sb_][:, sh_ * 256:(sh_ + 1) * 256], trim[:, jt * 256:(jt + 1) * 256],
                         ALU.mult, reads=(skey, "trim"), writes=(("ptr2", pi),))

              def ret_s2(n, c):
                  csl = slice(n * 256, (n + 1) * 256)
                  yh = c % 2
                  yps = PB[4 + yh][:, 0:256]
                  ykey = ("ps", 4 + yh)
                  pairs = [(rv3[:, 2 * n + jt, :], ptr2[(c % 2) * 2 + jt][:, :]) for jt in range(2)]
                  if n > 0:
                      pairs.append((sl4[:, h, n, :], rqb[b_][:, csl]))
                  pairs.append((sinb3[:, n, :], rqb[b_][:, csl]))
                  mmgroup(yps, pairs, reads=(("rvb", b_), ("ptr2", (c % 2) * 2), ("ptr2", (c % 2) * 2 + 1),
                                             ("sloc", h), "sinb", ("rqb", b_)), writes=(ykey,))
                  act(ysq[yh][:, :], yps, AF.Square, reads=(ykey,), writes=(("ysq", yh),))
                  sps = PB[6 + yh][:, 0:256]
                  mmgroup(sps, [(ones_bf, ysq[yh][:, :])], reads=(("ysq", yh), "cm"), writes=(("ps", 6 + yh),))
                  ts("dve", rst[yh][:, :], sps, 1.0 / 128, EPS, ALU.mult, ALU.add, reads=(("ps", 6 + yh),),
                     writes=(("rst", yh),))
                  act(rst[yh][:, :], rst[yh][:, :], AF.Sqrt, reads=(("rst", yh),), writes=(("rst", yh),))
                  P.op("dve", lambda e, yh=yh: e.reciprocal(out=rst[yh][:, :], in_=rst[yh][:, :]), reads=(("rst", yh),),
                       writes=(("rst", yh),))
                  stt("dve", ytm[yh][:, :], yps, gains[:, gcol:gcol + 1], rst[yh][:, :], ALU.mult, ALU.mult,
                      reads=(ykey, ("rst", yh), "gains"), writes=(("ytm", yh),))
                  tt("dve", mix3[:, 4 + h, csl], ytm[yh][:, :], rgb[b_][:, csl], ALU.mult,
                     reads=(("ytm", yh), ("rgb", b_)), writes=(("mix", 4 + h),))

              cbase = ci[0]
              ci[0] += 8
              ret_s1(0, cbase)
              for n in range(8):
                  if n + 1 < 8:
                      ret_s1(n + 1, cbase + n + 1)
                  ret_s2(n, cbase + n)

          chk("B2")
          P.barrier()
          A.o = AB
          kaug = [A.get(nm("kaug"), 4096, BF16) for _ in range(2)]
          vaug = [A.get(nm("vaug"), 32 * 128, BF16) for _ in range(2)]
          qaug = [A.get(nm("qaug"), NT, BF16) for _ in range(2)]
          ptr = [A.get(nm("ptr"), 512, BF16) for _ in range(4)]
          gaddt = A.get(nm("gaddt"), 256, F32)
          keept = A.get(nm("keept"), 256, F32)
          dmk = A.get(nm("dmk"), 4 * 256, BF16)
          gm = A.get(nm("gm"), 256, F32)
          g1 = A.get(nm("g1"), 256, F32)
          eq = A.get(nm("eq"), 256, F32)
          mx = A.get(nm("mx"), 16, F32)
          ksum = A.get(nm("ksum"), 16, F32)
          kmb = A.get(nm("kmb"), 16, BF16)
          mb = A.get(nm("mb"), 16 * 80, BF16)
          rec = [A.get(nm("rec"), 256, F32) for _ in range(2)]
          mb3 = mb[:, :].rearrange("p (q c) -> p q c", q=16)
          P.dma("sp", gaddt[:, :], gadd_d[:, :], writes=("gaddt",))
          P.dma("sp", keept[:, :], keepn_d[:, :], writes=("keept",))
          P.dma("sp", dmk[:, :], dmask_d[:, :], writes=("dmk",))
          P.op("dve", lambda en: en.memset(mb[:, :], 0.0), writes=("mb",))
          for b_ in range(2):
              P.dma("sp", kaug[b_][64:80, :], ind_d[:, :], writes=(("kaug", b_),))
              P.op("dve", lambda en, b_=b_: en.memset(vaug[b_][:, :], 1.0), writes=(("vaug", b_),))

          def moba_load(h):
              b_ = h % 2
              pp, par = h // 2, h % 2
              for r in range(2):
                  P.dma("sp", kaug[b_][0:64, r * NT:(r + 1) * NT],
                        AGK[r * 512 + pp * 128 + par * 64: r * 512 + pp * 128 + par * 64 + 64, :],
                        reads=("AGK",), writes=(("kaug", b_),))
                  P.dma("sp", vaug[b_][:, :].rearrange("p (t c) -> p t c", t=32)[:, r * 16:(r + 1) * 16,
                                                                                   par * 64:par * 64 + 64],
                        AGV[r * 1024 + h * 128: r * 1024 + (h + 1) * 128, :].rearrange("p (t c) -> p t c", t=16),
                        reads=("AGV",), writes=(("vaug", b_),))
              P.dma("sp", qaug[b_][0:64, :], QT[pp * 128 + par * 64: pp * 128 + par * 64 + 64, :],
                    reads=(), writes=(("qaug", b_),))

          def vaug_ap(b_, T, par):
              return vaug[b_][:, T * 128:(T + 1) * 128]

          def gate1(hh):
              bb = hh % 2
              kqq = ("kaug", bb)
              qqq = ("qaug", bb)
              P.op("dve", lambda en, b_=bb: en.tensor_reduce(
                  out=ksum[0:64, :], in_=kaug[b_][0:64, :].rearrange("p (n c) -> p n c", n=16), axis=AX.X, op=ALU.add),
                  reads=(kqq,), writes=("ksum",))
              P.op("dve", lambda en: en.tensor_copy(out=kmb[0:64, :], in_=ksum[0:64, :]), reads=("ksum",),
                   writes=("kmb",))
              fns = [(lambda en, qt=qt, b_=bb: en.matmul(PB[0][:, qt * 16:(qt + 1) * 16],
                                                        qaug[b_][0:64, qt * 128:(qt + 1) * 128], kmb[0:64, :],
                                                        start=True, stop=True)) for qt in range(16)]
              P.group("pe", fns, reads=(qqq, "kmb"), writes=(("ps", 0),))
              tt("dve", gm[:, :], PB[0][:, 0:256], gaddt[:, :], ALU.add, reads=(("ps", 0), "gaddt"), writes=("gm",))

              def g3(t):
                  return t[:, :].rearrange("p (q n) -> p q n", q=16)

              def mxb():
                  return mx[:, :].unsqueeze(2).to_broadcast([128, 16, 16])

              cur = gm
              ckey = "gm"
              for it in range(3):
                  P.op("dve", lambda en, cur=cur: en.tensor_reduce(out=mx[:, :], in_=g3(cur), axis=AX.X, op=ALU.max),
                       reads=(ckey,), writes=("mx",))
                  if it == 2:
                      break
                  P.op("dve", lambda en, cur=cur: en.tensor_tensor(out=g3(eq), in0=g3(cur), in1=mxb(), op=ALU.is_ge),
                       reads=(ckey, "mx"), writes=("eq",))
                  stt("dve", g1[:, :], eq[:, :], NEG, cur[:, :], ALU.mult, ALU.add, reads=("eq", ckey),
                      writes=("g1",))
                  cur = g1
                  ckey = "g1"
              ts("dve", mx[:, :], mx[:, :], -1.0e20, None, ALU.max, ALU.bypass, reads=("mx",), writes=("mx",))
              P.op("dve", lambda en: en.tensor_tensor(out=g3(eq), in0=g3(gm), in1=mxb(), op=ALU.is_lt),
                   reads=("gm", "mx"), writes=("eq",))
              P.op("dve", lambda en: en.tensor_tensor(out=mb3[:, :, 64:80], in0=g3(eq), in1=g3(keept), op=ALU.mult),
                   reads=("eq", "keept"), writes=("mb",))

          def gate2(hh):
              bb = hh % 2
              qqq = ("qaug", bb)
              for half in range(2):
                  fns = [(lambda en, j=j, half=half: en.matmul(
                      PB[(0, 2)[j // 4]][0:80, (j % 4) * 128:(j % 4 + 1) * 128], mb3[:, half * 8 + j, :], ident,
                      start=True, stop=True)) for j in range(8)]
                  P.group("pe", fns, reads=("mb", "cm"), writes=(("ps", 0), ("ps", 2)))
                  for j2 in range(2):
                      c0 = half * 1024 + j2 * 512
                      P.op("dve", lambda en, j2=j2, c0=c0, b_=bb: en.tensor_copy(
                          out=qaug[b_][64:80, c0:c0 + 512], in_=PB[(0, 2)[j2]][64:80, :]),
                          reads=(("ps", (0, 2)[j2]),), writes=(qqq,))

          moba_load(0)
          gate1(0)
          gate2(0)
          sring = [(4, 0), (5, 0), (6, 0), (7, 0)]
          for h in range(8):
              b_ = h % 2
              pp, par = h // 2, h % 2
              if h + 1 < 8:
                  moba_load(h + 1)
              kq = ("kaug", b_)
              qq = ("qaug", b_)
              for qb in range(8):
                  if h + 1 < 8 and qb == 2:
                      gate1(h + 1)
                  if h + 1 < 8 and qb == 5:
                      gate2(h + 1)
                  ob = 3 if qb % 2 == 0 else 1
                  ops_ = PB[ob][:, 0:256]
                  okey = ("ps", ob)
                  blocks = [(r, n) for r in range(2) for n in (range(8) if r == 0 else range(qb + 1))]
                  nbk = len(blocks)
                  qsl = slice(qb * 256, (qb + 1) * 256)

                  def s_mm(i):
                      r, n = blocks[i]
                      sb_ = sring[i % 4][0]
                      fns = []
                      for kt in range(2):
                          T = r * 16 + n * 2 + kt
                          fns.append(lambda en, T=T, kt=kt, sb_=sb_: en.matmul(
                              PB[sb_][:, kt * 256:(kt + 1) * 256], kaug[b_][0:80, T * 128:(T + 1) * 128],
                              qaug[b_][0:80, qsl], start=True, stop=True))
                      P.group("pe", fns, reads=(kq, qq), writes=(("ps", sb_),))

                  def pv(i):
                      r, n = blocks[i]
                      sb_ = sring[i % 4][0]
                      pi = i % 4
                      act(ptr[pi][:, :], PB[sb_][:, :], AF.Exp, reads=(("ps", sb_),), writes=(("ptr", pi),),
                          scale=0.125)
                      if n == qb:
                          tt("dve", ptr[pi][:, :], ptr[pi][:, :], dmk[:, r * 512:(r + 1) * 512], ALU.mult,
                             reads=(("ptr", pi), "dmk"), writes=(("ptr", pi),))
                      for kt in range(2):
                          T = r * 16 + n * 2 + kt
                          va = vaug_ap(b_, T, par)
                          P.op("pe", lambda en, va=va, pi=pi, i=i, kt=kt: en.matmul(
                              ops_, va, ptr[pi][:, kt * 256:(kt + 1) * 256], start=(i == 0 and kt == 0),
                              stop=(i == nbk - 1 and kt == 1)),
                              reads=(("vaug", b_), ("ptr", pi)), writes=(okey,))

                  s_mm(0)
                  if nbk > 1:
                      s_mm(1)
                  for i in range(nbk):
                      if i + 2 < nbk:
                          s_mm(i + 2)
                      pv(i)
                  nr = slice(0, 64) if par == 0 else slice(64, 128)
                  sr = slice(64, 128) if par == 0 else slice(0, 64)
                  rc = rec[qb % 2]
                  P.op("dve", lambda en, rc=rc, nr=nr, sr=sr: en.reciprocal(out=rc[nr, :], in_=ops_[sr, :]),
                       reads=(okey,), writes=(("rec", qb % 2),))
                  tt("dve", mix3[nr, pp, qsl], ops_[nr, :], rc[nr, :], ALU.mult, reads=(okey, ("rec", qb % 2)),
                     writes=(("mix", pp),))

          chk("B1")
          P.barrier()
          ri = [0]

          def resid_add(ps_ap, pkey, oc, tsl):
              tt("dve", hT3[:, oc, tsl], ps_ap, hT3[:, oc, tsl], ALU.add, reads=(pkey, ("h", oc)), writes=(("h", oc),))

          for hf in range(2):
              s = wensure(WIDX[(l, "out", hf)])
              W = wslot_view(s, 8, 512)
              for sub in range(4):
                  oc = hf * 4 + sub
                  for tg in range(4):
                      b = ri[0] % 7
                      ri[0] += 1
                      tsl = slice(tg * 512, (tg + 1) * 512)
                      mmgroup(PB[b][:, :], [(W[:, kc, sub * 128:(sub + 1) * 128], mix3[:, kc, tsl]) for kc in range(8)],
                              reads=[("mix", kc) for kc in range(8)] + [("w", s)], writes=(("ps", b),))
                      resid_add(PB[b][:, :], ("ps", b), oc, tsl)
          dump(f"hmix{l}")

          chk("C")
          P.barrier()
          A = Arena()
          nT = A.get(nm("nT"), 8 * NT, BF16)
          nT3 = nT[:, :].rearrange("p (k t) -> p k t", k=8)
          actT = A.get(nm("actT"), 22 * 1024, BF16)
          act3 = actT[:, :].rearrange("p (k t) -> p k t", k=22)
          sgt = [A.get(nm("sgt"), 512, F32) for _ in range(2)]
          rmsnorm(A, nT3, 16 + l * 8)
          si = [0]
          for th in range(2):
              for j in range(11):
                  s = wensure(WIDX[(l, "fi", th, j)])
                  W = wslot_view(s, 8, 512)
                  for sub in range(2):
                      for tgl in range(2):
                          tsl = slice(th * 1024 + tgl * 512, th * 1024 + (tgl + 1) * 512)
                          bg = ri[0] % 7
                          bu = (ri[0] + 1) % 7
                          ri[0] += 2
                          mmgroup(PB[bg][:, :], [(W[:, kc, sub * 128:(sub + 1) * 128], nT3[:, kc, tsl]) for kc in range(8)],
                                  reads=nkeys + [("w", s)], writes=(("ps", bg),))
                          mmgroup(PB[bu][:, :], [(W[:, kc, 256 + sub * 128:256 + (sub + 1) * 128], nT3[:, kc, tsl])
                                                 for kc in range(8)],
                                  reads=nkeys + [("w", s)], writes=(("ps", bu),))
                          sg = sgt[si[0] % 2]
                          sk = ("sgt", si[0] % 2)
                          si[0] += 1
                          act(sg[:, :], PB[bg][:, :], AF.Silu, reads=(("ps", bg),), writes=(sk,))
                          tt("dve", act3[:, j * 2 + sub, tgl * 512:(tgl + 1) * 512], PB[bu][:, :], sg[:, :], ALU.mult,
                             reads=(("ps", bu), sk), writes=(("actT", j * 2 + sub),))
              for oc in range(8):
                  s = wensure(WIDX[(l, "fo", th, oc)])
                  W = wslot_view(s, 22, 128)
                  for tgl in range(2):
                      tsl = slice(th * 1024 + tgl * 512, th * 1024 + (tgl + 1) * 512)
                      b = ri[0] % 7
                      ri[0] += 1
                      mmgroup(PB[b][:, :], [(W[:, kc, :], act3[:, kc, tgl * 512:(tgl + 1) * 512]) for kc in range(22)],
                              reads=[("actT", kc) for kc in range(22)] + [("w", s)], writes=(("ps", b),))
                      resid_add(PB[b][:, :], ("ps", b), oc, tsl)
          dump(f"hffn{l}")

          chk("FFN")
          P.barrier()
          A = Arena()
          nT = A.get(nm("nT"), 8 * NT, BF16)
          nT3 = nT[:, :].rearrange("p (k t) -> p k t", k=8)
          pTs = A.get(nm("pTs"), 2 * NT, BF16)
          pT3 = pTs[:, :].rearrange("p (k t) -> p k t", k=2)
          sgt = [A.get(nm("sgp"), 512, F32) for _ in range(2)]
          tpt = [A.get(nm("tpt"), 512, F32) for _ in range(2)]
          wpt = A.get(nm("wpt"), 2 * 1024, BF16)
          Wp = wpt[:, :].rearrange("p (k c) -> p k c", k=2)
          P.dma("pool", Wp, w_pp[l, :, :].rearrange("(k p) c -> p k c", p=128), writes=("wpt",))
          P.dma("pool", pT3, pT_d[l, :, :].rearrange("(k p) t -> p k t", p=128), writes=("pTs",))
          rmsnorm(A, nT3, 32 + l * 8)
          for hf in range(2):
              s = wensure(WIDX[(l, "pg", hf)])
              W = wslot_view(s, 8, 512)
              for sub in range(4):
                  oc = hf * 4 + sub
                  for tg in range(4):
                      tsl = slice(tg * 512, (tg + 1) * 512)
                      bg = ri[0] % 7
                      bu = (ri[0] + 1) % 7
                      ri[0] += 2
                      mmgroup(PB[bg][:, :], [(W[:, kc, sub * 128:(sub + 1) * 128], nT3[:, kc, tsl]) for kc in range(8)],
                              reads=nkeys + [("w", s)], writes=(("ps", bg),))
                      mmgroup(PB[bu][:, :], [(Wp[:, k2, oc * 128:(oc + 1) * 128], pT3[:, k2, tsl]) for k2 in range(2)],
                              reads=("pTs", "wpt"), writes=(("ps", bu),))
                      i2 = si[0] % 2
                      si[0] += 1
                      act(sgt[i2][:, :], PB[bg][:, :], AF.Sigmoid, reads=(("ps", bg),), writes=(("sgp", i2),))
                      tt("dve", tpt[i2][:, :], PB[bu][:, :], sgt[i2][:, :], ALU.mult, reads=(("ps", bu), ("sgp", i2)),
                         writes=(("tpt", i2),))
                      tt("dve", hT3[:, oc, tsl], tpt[i2][:, :], hT3[:, oc, tsl], ALU.add,
                         reads=(("tpt", i2), ("h", oc)), writes=(("h", oc),))
          dump(f"hple{l}")

    except _Stop:
        pass

    P.barrier()
    A = Arena()
    if last:
        rmsnorm(A, None, 48, out_f32_dram=outT)
    else:
        for kc in range(8):
            P.dma("sp", outT[kc * 128:(kc + 1) * 128, :], hT3[:, kc, :], reads=(("h", kc),))
    P.final_wait("sp")

    semnames = P.semnames + ["pool_cc"]
    import contextlib
    with contextlib.ExitStack() as st:
        sems = {n: st.enter_context(nc.semaphore(n)) for n in semnames}
        block = st.enter_context(nc.Block())

        @block.tensor
        def _(e):
            P.replay("pe", e, sems)

        @block.scalar
        def _(e):
            P.replay("act", e, sems)

        @block.vector
        def _(e):
            P.replay("dve", e, sems)

        @block.gpsimd
        def _(e):
            P.replay("pool", e, sems)

        @block.sync
        def _(e):
            P.replay("sp", e, sems)
    return nc


def _tables(half):
    f32 = np.float32
    pos = (half * NT + np.arange(NT)).astype(f32)
    inv = (1.0 / (f32(10000.0) ** np.linspace(0.0, 1.0, 64, dtype=f32))).astype(f32)
    ang = (pos[:, None] * inv[None, :]).astype(f32)
    cosT = np.ascontiguousarray(np.cos(ang).astype(f32).T[np.arange(128) // 2])
    sinT = np.ascontiguousarray(np.sin(ang).astype(f32).T[np.arange(128) // 2])
    logg = np.log1p(-np.exp2(-5.0 - np.arange(4, dtype=np.float64)))
    i = np.arange(256, dtype=np.float64)
    dec = np.zeros((8, 256), np.float64)
    for h in range(4):
        dec[h] = np.exp((i + 1.0) * logg[h])
        dec[4 + h] = np.exp(-(i + 1.0) * logg[h]) * (128.0 ** -0.5)
    dectab = np.ascontiguousarray(np.broadcast_to(dec.reshape(1, -1), (128, 2048))).astype(f32)
    gadd = np.zeros((16, 16), f32)
    keepn = np.full((16, 16), MBIG, f32)
    for qt in range(16):
        bg = half * 8 + qt // 2
        gadd[qt, bg:] = NEG
        keepn[qt, bg] = 0.0
    gadd = np.ascontiguousarray(np.broadcast_to(gadd.reshape(1, -1), (128, 256)))
    keepn = np.ascontiguousarray(np.broadcast_to(keepn.reshape(1, -1), (128, 256)))
    k = np.arange(128)[:, None]
    q = np.arange(256)[None, :]
    tri = [(kt * 128 + k <= q).astype(f32) for kt in range(2)]
    trimask = np.concatenate(tri, axis=1)
    dm = []
    for r in range(2):
        for kt in range(2):
            if r == half:
                dm.append(tri[kt])
            elif r < half:
                dm.append(np.ones((128, 256), f32))
            else:
                dm.append(np.zeros((128, 256), f32))
    dmask = np.concatenate(dm, axis=1).astype(ml_dtypes.bfloat16)
    ind = np.zeros((16, 4096), f32)
    for n in range(16):
        ind[n, n * 256:(n + 1) * 256] = 1.0
    ident = np.eye(128, dtype=f32)
    ones = np.ones((128, 128), f32)
    jm = np.zeros((128, 128), f32)
    for a in range(64):
        jm[2 * a + 1, 2 * a] = -1.0
        jm[2 * a, 2 * a + 1] = 1.0
    cm = np.concatenate([ident, ones, jm], axis=1).astype(ml_dtypes.bfloat16)
    flag = np.full((128, 1), float(half), f32)
    return dict(cosT=cosT, sinT=sinT, dectab=dectab, gadd=gadd, keepneg=keepn, dmask=dmask, trimask=trimask,
                indrows=ind.astype(ml_dtypes.bfloat16), cmats=cm, flag=flag)


def _gains(attn_norm_g, ffn_norm_g, ple_norm_g, final_norm_g, ret_norm_g):
    g = np.zeros((128, 64), np.float32)
    for l in range(DEPTH):
        g[:, l * 8:(l + 1) * 8] = attn_norm_g[l].reshape(8, 128).T
        g[:, 16 + l * 8:16 + (l + 1) * 8] = ffn_norm_g[l].reshape(8, 128).T
        g[:, 32 + l * 8:32 + (l + 1) * 8] = ple_norm_g[l].reshape(8, 128).T
        g[:, 56 + l * 4:56 + (l + 1) * 4] = ret_norm_g[l].reshape(4, 128).T
    g[:, 48:56] = final_norm_g.reshape(8, 128).T
    return g


_CACHE = {}


def _run(layers, first, last, xTs, shared, dbg=()):
    key = (tuple(layers), first, last, tuple(dbg))
    if key not in _CACHE:
        _CACHE[key] = build_program(list(layers), first, last, dbg)
    nc = _CACHE[key]
    in_maps = []
    for c in range(8):
        m = dict(shared)
        m.update(_tables(c % 2))
        m["xT"] = xTs[c]
        m["pT"] = shared["pT_all"][c]
        del m["pT_all"]
        in_maps.append(m)
    ncores = int(os.environ.get("KCORES", "8"))
    res = run_bass_kernel_spmd(nc, in_maps[:ncores], core_ids=list(range(ncores)))
    rr = list(res.results)
    while len(rr) < 8:
        rr.append(rr[0])
    return rr


def kernel(x, p, attn_norm_g, w_in, ret_norm_g, w_out, ffn_norm_g, w_ffn_in, w_ffn_out, ple_norm_g,
           w_ple_gate, w_ple_proj, final_norm_g, _dbg=(), _split=False):
    f32 = np.float32
    x = np.asarray(x, f32)
    p = np.asarray(p, f32)
    xTs, pTs = [], []
    for c in range(8):
        b, hf = c // 2, c % 2
        xTs.append(np.ascontiguousarray(x[b, hf * NT:(hf + 1) * NT, :].T))
        pTs.append(np.ascontiguousarray(p[:, b, hf * NT:(hf + 1) * NT, :].transpose(0, 2, 1)))
    shared = dict(
        w_in=np.asarray(w_in, f32), w_out=np.asarray(w_out, f32), w_ffn_in=np.asarray(w_ffn_in, f32),
        w_ffn_out=np.asarray(w_ffn_out, f32), w_ple_gate=np.asarray(w_ple_gate, f32),
        w_ple_proj=np.asarray(w_ple_proj, f32),
        gains=_gains(np.asarray(attn_norm_g, f32), np.asarray(ffn_norm_g, f32), np.asarray(ple_norm_g, f32),
                     np.asarray(final_norm_g, f32), np.asarray(ret_norm_g, f32)),
        pT_all=pTs,
    )
    if _split:
        r0 = _run([0], True, False, xTs, shared)
        xT1 = [np.ascontiguousarray(r["outT"]) for r in r0]
        res = _run([1], False, True, xT1, shared)
    else:
        res = _run([0, 1], True, True, xTs, shared, dbg=_dbg)
    out = np.zeros((4, 4096, D), f32)
    for c in range(8):
        b, hf = c // 2, c % 2
        out[b, hf * NT:(hf + 1) * NT, :] = res[c]["outT"].T
    if _dbg:
        return out, res
    return out
```
